# Optimizing a Trainium2 kernel written in Bass

```python
import math
import jax, jax.numpy as jnp
from jax import lax
import numpy as np

D_MODEL = 4096
BATCH = 4
SEQ = 2048
DEPTH = 1

MEM_LEN = 256
MLA_HEADS = 16
MLA_Q_RANK = 1024
MLA_KV_RANK = 512
MLA_NOPE = 128
MLA_ROPE = 64
MLA_V = 128
Q_BLOCK = 128
RET_HEADS = 8
RET_DK = 256
RET_DV = 512
RET_CHUNK = 128
X_HEADS = 4
X_HEAD_DIM = D_MODEL // X_HEADS
D_FF = 11008
CONV_W = 3
ROPE_THETA = 10000.0
LN_EPS = 1e-5
RMS_EPS = 1e-6
ALPHA = (2 * DEPTH) ** 0.25
BETA = (8 * DEPTH) ** -0.25
N_BRANCHES = 2

IN_SIZES = (MLA_Q_RANK, MLA_KV_RANK, MLA_ROPE,
            RET_HEADS * RET_DK, RET_HEADS * RET_DK, RET_HEADS * RET_DV, RET_HEADS * RET_DV,
            N_BRANCHES * D_MODEL)
IN_WIDTH = int(sum(IN_SIZES))
IN_SPLITS = [int(v) for v in np.cumsum(IN_SIZES)[:-1]]

kernel_name = 'hybrid_mla_retention_encoder_block'


def _layer_norm(x, g, b):
    xf = x.astype(jnp.float32)
    mu = jnp.mean(xf, axis=-1, keepdims=True)
    var = jnp.mean(jnp.square(xf - mu), axis=-1, keepdims=True)
    y = (xf - mu) * lax.rsqrt(var + LN_EPS)
    return (y * g.astype(jnp.float32) + b.astype(jnp.float32)).astype(x.dtype)


def _rms_norm(x, g):
    xf = x.astype(jnp.float32)
    y = xf * lax.rsqrt(jnp.mean(jnp.square(xf), axis=-1, keepdims=True) + RMS_EPS)
    return (y * g.astype(jnp.float32)).astype(x.dtype)


def _group_norm(o):
    of = o.astype(jnp.float32)
    mu = jnp.mean(of, axis=-1, keepdims=True)
    var = jnp.mean(jnp.square(of - mu), axis=-1, keepdims=True)
    return ((of - mu) * lax.rsqrt(var + LN_EPS)).astype(o.dtype)


def _rope(t, positions):
    d = t.shape[-1]
    half = d // 2
    inv = ROPE_THETA ** (-jnp.arange(half, dtype=jnp.float32) / half)
    ang = positions.astype(jnp.float32)[:, :, None] * inv
    cos = jnp.cos(ang)[:, :, None, :].astype(t.dtype)
    sin = jnp.sin(ang)[:, :, None, :].astype(t.dtype)
    t1, t2 = t[..., :half], t[..., half:]
    return jnp.concatenate([t1 * cos - t2 * sin, t2 * cos + t1 * sin], axis=-1)


def _blocked_attention(q, k, v, scale):
    B, S, H, dq = q.shape
    dv = v.shape[-1]
    nb = S // Q_BLOCK
    qb = q.reshape(B, nb, Q_BLOCK, H, dq).transpose(1, 0, 2, 3, 4)

    def one_block(q_blk):
        s = jnp.einsum('bqhd,bkhd->bhqk', q_blk, k).astype(jnp.float32) * scale
        p = jax.nn.softmax(s, axis=-1).astype(v.dtype)
        return jnp.einsum('bhqk,bkhe->bqhe', p, v)

    o = lax.map(one_block, qb)
    return o.transpose(1, 0, 2, 3, 4).reshape(B, S, H, dv)


def _log_decay(e):
    return jnp.log1p(-jnp.exp2(-e.astype(jnp.float32)))


def _retention_scan(q, k, v, log_g, strict):
    B, H, S, dk = q.shape
    dv = v.shape[-1]
    nc = S // RET_CHUNK
    dt = q.dtype
    idx = jnp.arange(RET_CHUNK, dtype=jnp.float32)
    diff = idx[:, None] - idx[None, :]
    mask = (diff > 0) if strict else (diff >= 0)
    decay_in = jnp.where(mask[None], jnp.exp(log_g[:, None, None] * jnp.where(mask, diff, 0.0)[None]), 0.0).astype(dt)
    xi = jnp.exp(log_g[:, None] * (idx + 1.0)).astype(dt)[None, :, :, None]
    zeta = jnp.exp(log_g[:, None] * (RET_CHUNK - 1.0 - idx)).astype(dt)[None, :, :, None]
    g_chunk = jnp.exp(log_g * RET_CHUNK).astype(dt)[None, :, None, None]

    def to_chunks(t):
        return jnp.moveaxis(t.reshape(B, H, nc, RET_CHUNK, t.shape[-1]), 2, 0)

    def step(state, inp):
        qc, kc, vc = inp
        scores = jnp.einsum('bhnd,bhmd->bhnm', qc, kc) * decay_in[None]
        o = jnp.einsum('bhnm,bhme->bhne', scores, vc) + jnp.einsum('bhnd,bhde->bhne', qc, state) * xi
        state = state * g_chunk + jnp.einsum('bhmd,bhme->bhde', kc * zeta, vc)
        return state, o

    state0 = jnp.zeros((B, H, dk, dv), dtype=dt)
    _, o = lax.scan(step, state0, (to_chunks(q), to_chunks(k), to_chunks(v)))
    return jnp.moveaxis(o, 0, 2).reshape(B, H, S, dv)


def _dwconv_centred(u, w, b):
    up = jnp.pad(u, ((0, 0), (1, 1), (0, 0)))
    return up[:, :-2] * w[0] + up[:, 1:-1] * w[1] + up[:, 2:] * w[2] + b


def setup_inputs(seed: int = 0) -> dict:
    key = jax.random.key(seed)
    ks = jax.random.split(key, 32)
    f32 = jnp.float32

    def w(k, shape, fan_in, scale=1.0):
        return jax.random.normal(k, shape, f32) * (fan_in ** -0.5) * scale

    def gain(k, shape):
        return 1.0 + 0.02 * jax.random.normal(k, shape, f32)

    def bias(k, shape):
        return 0.01 * jax.random.normal(k, shape, f32)

    L = DEPTH
    x = jax.random.normal(ks[0], (BATCH, SEQ, D_MODEL), f32)
    mem = jax.random.normal(ks[1], (BATCH, MEM_LEN, D_MODEL), f32)
    positions = jax.random.randint(ks[2], (BATCH, 1), 0, 1024, dtype=jnp.int32) + jnp.arange(SEQ, dtype=jnp.int32)[None, :]
    decay_base = 5.0 + jnp.arange(RET_HEADS, dtype=f32)
    return {
        'x': x,
        'mem': mem,
        'positions': positions,
        'w_in': w(ks[3], (L, D_MODEL, IN_WIDTH), D_MODEL),
        'gate_bias': bias(ks[4], (L, N_BRANCHES, D_MODEL)),
        'q_norm_g': gain(ks[5], (L, MLA_Q_RANK)),
        'w_uq': w(ks[6], (L, MLA_Q_RANK, MLA_HEADS * (MLA_NOPE + MLA_ROPE)), MLA_Q_RANK),
        'kv_norm_g': gain(ks[7], (L, MLA_KV_RANK)),
        'w_ukv': w(ks[8], (L, MLA_KV_RANK, MLA_HEADS * (MLA_NOPE + MLA_V)), MLA_KV_RANK),
        'ret_decay_fwd': decay_base + 0.1 * jax.random.normal(ks[9], (L, RET_HEADS), f32),
        'ret_decay_bwd': decay_base + 0.1 * jax.random.normal(ks[10], (L, RET_HEADS), f32),
        'w_br_mla': w(ks[11], (L, MLA_HEADS * MLA_V, D_MODEL), MLA_HEADS * MLA_V),
        'w_br_ret': w(ks[12], (L, RET_HEADS * RET_DV, D_MODEL), RET_HEADS * RET_DV),
        'w_o': w(ks[13], (L, D_MODEL, D_MODEL), D_MODEL, BETA),
        'ln1_g': gain(ks[14], (L, D_MODEL)),
        'ln1_b': bias(ks[15], (L, D_MODEL)),
        'w_cq': w(ks[16], (L, D_MODEL, D_MODEL), D_MODEL),
        'w_ck': w(ks[17], (L, D_MODEL, D_MODEL), D_MODEL),
        'w_cv': w(ks[18], (L, D_MODEL, D_MODEL), D_MODEL),
        'w_co': w(ks[19], (L, D_MODEL, D_MODEL), D_MODEL, BETA),
        'ln2_g': gain(ks[20], (L, D_MODEL)),
        'ln2_b': bias(ks[21], (L, D_MODEL)),
        'w_ffn_in': w(ks[22], (L, D_MODEL, 2 * D_FF), D_MODEL),
        'conv_w': w(ks[23], (L, CONV_W, 2 * D_FF), CONV_W),
        'conv_b': bias(ks[24], (L, 2 * D_FF)),
        'w_ffn_out': w(ks[25], (L, D_FF, D_MODEL), D_FF, BETA),
        'ln3_g': gain(ks[26], (L, D_MODEL)),
        'ln3_b': bias(ks[27], (L, D_MODEL)),
    }


def reference(x, mem, positions, w_in, gate_bias, q_norm_g, w_uq, kv_norm_g, w_ukv,
              ret_decay_fwd, ret_decay_bwd, w_br_mla, w_br_ret, w_o, ln1_g, ln1_b,
              w_cq, w_ck, w_cv, w_co, ln2_g, ln2_b, w_ffn_in, conv_w, conv_b,
              w_ffn_out, ln3_g, ln3_b):
    B, S, _ = x.shape
    M = mem.shape[1]
    h = x
    for l in range(DEPTH):
        proj = h @ w_in[l]
        c_q, c_kv, k_r, r_q, r_k, r_v, r_g, gates = jnp.split(proj, IN_SPLITS, axis=-1)

        q = (_rms_norm(c_q, q_norm_g[l]) @ w_uq[l]).reshape(B, S, MLA_HEADS, MLA_NOPE + MLA_ROPE)
        q_nope, q_pe = q[..., :MLA_NOPE], _rope(q[..., MLA_NOPE:], positions)
        kv = (_rms_norm(c_kv, kv_norm_g[l]) @ w_ukv[l]).reshape(B, S, MLA_HEADS, MLA_NOPE + MLA_V)
        k_nope, v_mla = kv[..., :MLA_NOPE], kv[..., MLA_NOPE:]
        k_pe = _rope(k_r[:, :, None, :], positions)
        q_full = jnp.concatenate([q_nope, q_pe], axis=-1)
        k_full = jnp.concatenate([k_nope, jnp.broadcast_to(k_pe, (B, S, MLA_HEADS, MLA_ROPE))], axis=-1)
        a_out = _blocked_attention(q_full, k_full, v_mla, (MLA_NOPE + MLA_ROPE) ** -0.5)
        a_out = a_out.reshape(B, S, MLA_HEADS * MLA_V)

        rq = _rope(r_q.reshape(B, S, RET_HEADS, RET_DK), positions).transpose(0, 2, 1, 3)
        rk = (_rope(r_k.reshape(B, S, RET_HEADS, RET_DK), positions) * (RET_DK ** -0.5)).transpose(0, 2, 1, 3)
        rv = r_v.reshape(B, S, RET_HEADS, RET_DV).transpose(0, 2, 1, 3)
        o_fwd = _retention_scan(rq, rk, rv, _log_decay(ret_decay_fwd[l]), False)
        o_bwd = _retention_scan(rq[:, :, ::-1], rk[:, :, ::-1], rv[:, :, ::-1],
                                _log_decay(ret_decay_bwd[l]), True)[:, :, ::-1]
        o_ret = _group_norm(o_fwd + o_bwd).transpose(0, 2, 1, 3).reshape(B, S, RET_HEADS * RET_DV)
        r_out = jax.nn.silu(r_g) * o_ret

        g = jax.nn.sigmoid(gates.reshape(B, S, N_BRANCHES, D_MODEL) + gate_bias[l])
        mixed = g[:, :, 0] * (a_out @ w_br_mla[l]) + g[:, :, 1] * (r_out @ w_br_ret[l])
        h = _layer_norm(ALPHA * h + mixed @ w_o[l], ln1_g[l], ln1_b[l])

        cq = (h @ w_cq[l]).reshape(B, S, X_HEADS, X_HEAD_DIM)
        ck = (mem @ w_ck[l]).reshape(B, M, X_HEADS, X_HEAD_DIM)
        cv = (mem @ w_cv[l]).reshape(B, M, X_HEADS, X_HEAD_DIM)
        s = jnp.einsum('bqhd,bmhd->bhqm', cq, ck).astype(jnp.float32) * (X_HEAD_DIM ** -0.5)
        p = jax.nn.softmax(s, axis=-1).astype(cv.dtype)
        c_out = jnp.einsum('bhqm,bmhd->bqhd', p, cv).reshape(B, S, D_MODEL)
        h = _layer_norm(ALPHA * h + c_out @ w_co[l], ln2_g[l], ln2_b[l])

        u = _dwconv_centred(h @ w_ffn_in[l], conv_w[l], conv_b[l])
        up, gt = u[..., :D_FF], u[..., D_FF:]
        h = _layer_norm(ALPHA * h + (jax.nn.silu(gt) * up) @ w_ffn_out[l], ln3_g[l], ln3_b[l])
    return h
```

```python
import math
from contextlib import ExitStack

import numpy as np
import concourse.bass as bass
import concourse.mybir as mybir
from concourse.bass_utils import run_bass_kernel_spmd

F32 = mybir.dt.float32
BF = mybir.dt.bfloat16
I32 = mybir.dt.int32
AF = mybir.ActivationFunctionType
ALU = mybir.AluOpType

D = 4096
T = 1024
NTB = 2
DFF = 11008
NFF = DFF // 128
ALPHA = 2.0 ** 0.25
LN_EPS = 1e-5
RMS_EPS = 1e-6
PI = math.pi

V_GB = 0
V_QG = 64
V_KG = 72
V_LN = 76
V_CW = 268
V_CB = 784
NVEC = 956
C_RP, C_RM, C_N1, C_N2 = 0, 128, 256, 384
C_MC1, C_MC2, C_INV128, C_INV32, C_SGN, C_SEL0, C_SEL1 = 512, 513, 514, 515, 516, 517, 518
NCONST = 520

UPTO = 99
DEBUG = []


class Q:
    def __init__(self, name, sem):
        self.name = name
        self.sem = sem
        self.ops = []
        self.seen = {}


class Prog:
    def __init__(self, nc, es):
        self.nc = nc
        self.es = es
        self.sems = []
        self.semcnt = []
        self.q = {}
        for n in ("pe", "act", "dve", "pool", "sp"):
            self.q[n] = Q(n, self.new_sem())
        self.nbank = 8
        self.bank_free = [None] * 8
        self.bank_cur = 0
        self.rot_banks = list(range(8))
        self.deferred = []

    def new_sem(self):
        h = self.es.enter_context(self.nc.semaphore(f"s{len(self.sems)}"))
        self.sems.append(h)
        self.semcnt.append(0)
        return len(self.sems) - 1

    def wait(self, qn, tok):
        if tok is None:
            return
        q = self.q[qn]
        s, v = tok
        if qn == "pe" and s == q.sem:
            return
        if q.seen.get(s, 0) >= v:
            return
        q.seen[s] = v
        q.ops.append(("wait", s, v))

    def op(self, qn, fn, deps=(), sig=True):
        q = self.q[qn]
        for d in deps:
            self.wait(qn, d)
        tok = None
        if sig:
            self.semcnt[q.sem] += 1
            tok = (q.sem, self.semcnt[q.sem])
        q.ops.append(("op", fn, q.sem if sig else None, 1))
        return tok

    def dma(self, qn, out, in_, sem, deps=()):
        q = self.q[qn]
        for d in deps:
            self.wait(qn, d)
        self.semcnt[sem] += 16
        q.ops.append(("op", lambda e: e.dma_start(out=out, in_=in_), sem, 16))
        return (sem, self.semcnt[sem])

    def barrier(self, queues=("pe", "act", "dve", "sp")):
        for qn in queues:
            for s, c in enumerate(self.semcnt):
                if c > 0:
                    self.wait(qn, (s, c))

    def next_bank(self):
        b = self.rot_banks[self.bank_cur % len(self.rot_banks)]
        self.bank_cur += 1
        return b

    def run_deferred(self):
        d, self.deferred = self.deferred, []
        for f in d:
            f()

    def replay(self):
        nc = self.nc
        sems = self.sems

        def run(q, eng):
            for o in q.ops:
                if o[0] == "wait":
                    eng.wait_ge(sems[o[1]], o[2])
                else:
                    ins = o[1](eng)
                    if o[2] is not None:
                        ins.then_inc(sems[o[2]], o[3])

        with nc.Block() as block:
            @block.tensor
            def _(e):
                run(self.q["pe"], e)

            @block.scalar
            def _(e):
                run(self.q["act"], e)

            @block.vector
            def _(e):
                run(self.q["dve"], e)

            @block.gpsimd
            def _(e):
                run(self.q["pool"], e)

            @block.sync
            def _(e):
                run(self.q["sp"], e)


class TilePool:
    def __init__(self, P, aps, with_sems=True):
        self.P = P
        self.aps = aps
        self.sems = [P.new_sem() for _ in aps] if with_sems else None
        self.last = [None] * len(aps)
        self.held = [False] * len(aps)
        self.i = 0

    def get(self):
        i = self.i
        self.i = (i + 1) % len(self.aps)
        assert not self.held[i], "tile pool too small: tile still held"
        self.held[i] = True
        return i, self.aps[i], self.last[i]

    def release(self, i, tok):
        self.last[i] = tok
        self.held[i] = False

    def store(self, i, dram, src, dep):
        tok = self.P.dma("sp", dram, src, self.sems[i], deps=[dep])
        self.release(i, tok)
        return tok


def carve(region, off_bytes, shape, dtype):
    esz = 2 if dtype == BF else 4
    n = int(np.prod(shape[1:]))
    a = region[:, off_bytes // 2: off_bytes // 2 + n * esz // 2]
    if dtype != BF:
        a = a.bitcast(dtype)
    if len(shape) == 3:
        a = a.rearrange("p (a b) -> p a b", a=shape[1])
    elif len(shape) == 4:
        a = a.rearrange("p (a b c) -> p a b c", a=shape[1], b=shape[2])
    return a


def build(upto=99):
    nc = bass.Bass("TRN2", target_bir_lowering=False)
    es = ExitStack()
    P = Prog(nc, es)

    P.in_names = []

    def din(name, shape, dt=F32, need=0):
        if upto < need:
            return None
        P.in_names.append(name)
        return nc.dram_tensor(name, list(shape), dt, kind="ExternalInput").ap()

    def dscr(name, shape, dt):
        kind = "ExternalOutput" if name in DEBUG else "Internal"
        return nc.dram_tensor(name, list(shape), dt, kind=kind).ap()

    xT = din("xT", [D, T])
    memT = din("memT", [D, 256], need=6)
    pos_d = din("pos", [128, T], I32)
    const_d = din("const", [128, NCONST])
    vec_d = din("vec", [128, NVEC])
    dec_d = din("decay", [128, 16])
    w_in = din("w_in", [174, 128, 4096])
    w_uq = din("w_uq", [32, 128, 1024])
    w_ukv = din("w_ukv", [32, 128, 512])
    w_brm = din("w_brm", need=5, shape=[32, 128, 2048])
    w_brr = din("w_brr", need=5, shape=[32, 128, 4096])
    w_o = din("w_o", [32, 128, 4096], need=6)
    w_cq = din("w_cq", [32, 128, 4096], need=6)
    w_ck = din("w_ck", [32, 128, 4096], need=6)
    w_cv = din("w_cv", [32, 128, 4096], need=6)
    w_co = din("w_co", [32, 128, 4096], need=8)
    w_f1 = din("w_f1", [172, 128, 4096], need=10)
    w_f2 = din("w_f2", [32, 128, NFF * 128], need=11)
    outT = nc.dram_tensor("outT", [32, 128, T], F32, kind="ExternalOutput").ap()

    kv_src = dscr("kv_src", [4224, T], BF)
    KV_CH = [(0, 1024), (1024, 2048), (2048, 2176), (2176, 3200), (3200, 4224)]
    kv_dst = [dscr(f"kv_dst{i}", [2 * (b_ - a_), T], BF) for i, (a_, b_) in enumerate(KV_CH)]
    qnt = dscr("qnt", [16, 128, T], BF)
    qpt = dscr("qpt", [8, 128, T], BF)
    rqt = dscr("rqt", [16, 128, T], BF)
    rkt = dscr("rkt", [16, 128, T], BF)
    rvt = dscr("rvt", [32, 128, T], BF)
    rgt = dscr("rgt", [32, 128, T], BF)
    gtt = dscr("gtt", [64, 128, T], BF)
    sf_src = dscr("sf_src", [2048, 512], F32)
    sf_dst = [dscr(f"sf_dst{i}", [2048, 512], F32) for i in range(2)]
    aot = dscr("aot", [16, 128, T], BF)
    rot = dscr("rot", [32, 128, T], BF)
    mixt = dscr("mixt", [32, 128, T], BF)
    ysc = dscr("ysc", [32, 128, T], F32)
    h1s = dscr("h1s", [32, 128, T], F32)
    h2s = dscr("h2s", [32, 128, T], F32)
    cqt = dscr("cqt", [32, 128, T], BF)
    hsrc = dscr("hsrc", [128, 32], F32)
    hdst = dscr("hdst", [256, 32], F32)
    fft = dscr("fft", [NFF, 128, T], BF)

    def sb(name, shape, dt):
        return es.enter_context(nc.sbuf_tensor(name, list(shape), dt))

    RAB = sb("RAB", [128, 65536], BF)
    RA = RAB[:, 0:32768]
    RB = RAB[:, 32768:65536]
    RW = sb("RW", [128, 24576], BF)
    CONST = sb("CONST", [128, NCONST], F32)
    VEC = sb("VEC", [128, NVEC], F32)
    DEC = sb("DEC", [128, 16], F32)
    LG = sb("LG", [128, 16], F32)
    STAT = sb("STAT", [128, 2048], F32)
    IDENT = sb("IDENT", [128, 128], BF)
    ONESB = sb("ONESB", [128, 128], BF)
    ONESF = sb("ONESF", [128, 128], F32)
    NEGPI = sb("NEGPI", [128, 1], F32)
    SMALL = sb("SMALL", [128, 64], F32)
    FST = [sb(f"FST{i}", [128, 512], F32) for i in range(5)]
    BST = [sb(f"BST{i}", [128, 512], BF) for i in range(4)]
    PS = [es.enter_context(nc.psum_tensor(f"PS{i}", [128, 512], F32)) for i in range(8)]

    FSTP = TilePool(P, [t[:] for t in FST])
    BSTP = TilePool(P, [t[:] for t in BST])
    ldsem = [P.new_sem() for _ in range(12)]

    NSLOT, SLOT = 6, 4096
    wsem = [P.new_sem() for _ in range(NSLOT)]
    slot_free = [None] * NSLOT
    wcur = [0]

    def load_w(Wap, KC):
        k = (KC * 128 + SLOT - 1) // SLOT
        s0 = wcur[0]
        if k > 1:
            s0 = ((s0 + k - 1) // k) * k
        if s0 + k > NSLOT:
            s0 = 0
        wcur[0] = (s0 + k) % NSLOT
        deps = [slot_free[s] for s in range(s0, s0 + k)]
        dst = RW[:, s0 * SLOT: s0 * SLOT + KC * 128]
        tok = P.dma("pool", dst, Wap, wsem[s0], deps=deps)
        return s0, k, tok, dst

    dstate = {"old": []}

    def gemm_job(Wap, KC, groups):
        s0, k, wtok, wt = load_w(Wap, KC)
        last = None
        for (N, rhs_fn, evac, deps) in groups:
            b = P.next_bank()
            ps = PS[b][:, :N]
            P.wait("pe", wtok)
            P.wait("pe", P.bank_free[b])
            for d in deps:
                P.wait("pe", d)
            tok = None
            for kc in range(KC):
                lhsT = wt[:, kc * 128:(kc + 1) * 128]
                rhs = rhs_fn(kc)
                tok = P.op("pe", (lambda e, ps=ps, lhsT=lhsT, rhs=rhs, st=(kc == 0), sp=(kc == KC - 1):
                                  e.matmul(ps, lhsT, rhs, start=st, stop=sp)), sig=(kc == KC - 1))
            last = tok
            P.bank_free[b] = evac(ps, tok)
        for s in range(s0, s0 + k):
            slot_free[s] = last
        prev, dstate["old"] = dstate["old"], P.deferred
        P.deferred = []
        for f in prev:
            f()

    def flush_deferred():
        for f in dstate["old"] + P.deferred:
            f()
        dstate["old"] = []
        P.deferred = []

    def mm(ps, lhsT, rhs, st, sp, sig, deps=()):
        return P.op("pe", lambda e: e.matmul(ps, lhsT, rhs, start=st, stop=sp), deps=deps, sig=sig)

    def tr(ps, in_, deps=(), sig=True):
        return P.op("pe", lambda e: e.transpose(ps, in_, IDENT[:]), deps=deps, sig=sig)

    def act(out, in_, func, bias=None, scale=1.0, deps=()):
        if bias is None:
            return P.op("act", lambda e: e.activation(out=out, in_=in_, func=func, scale=scale), deps=deps)
        return P.op("act", lambda e: e.activation(out=out, in_=in_, func=func, bias=bias, scale=scale), deps=deps)

    def tt(q, out, in0, in1, op, deps=()):
        return P.op(q, lambda e: e.tensor_tensor(out=out, in0=in0, in1=in1, op=op), deps=deps)

    def ts(q, out, in0, s1, s2, op0, op1=None, deps=()):
        if op1 is None:
            return P.op(q, lambda e: e.tensor_scalar(out=out, in0=in0, scalar1=s1, scalar2=None, op0=op0), deps=deps)
        return P.op(q, lambda e: e.tensor_scalar(out=out, in0=in0, scalar1=s1, scalar2=s2, op0=op0, op1=op1), deps=deps)

    def stt(q, out, in0, scalar, in1, op0, op1, deps=()):
        return P.op(q, lambda e: e.scalar_tensor_tensor(out=out, in0=in0, scalar=scalar, in1=in1, op0=op0, op1=op1),
                    deps=deps)

    def load(dst, src, semi, deps=()):
        return P.dma("sp", dst, src, ldsem[semi], deps=deps)

    t_const = load(CONST[:], const_d, 0)
    t_vec = load(VEC[:], vec_d, 1)
    t_dec = load(DEC[:], dec_d, 2)
    POS = carve(RB, 0, [128, T], I32)
    t_pos = load(POS, pos_d, 3)
    HT = carve(RA, 0, [128, 32, T], BF)
    hsem = P.new_sem()
    xv = xT.rearrange("(kc p) t -> p kc t", p=128)
    t_ht = None
    for i in range(4):
        t_ht = P.dma("pool", HT[:, i * 8:(i + 1) * 8, :], xv[:, i * 8:(i + 1) * 8, :], hsem)
    t_ht = (hsem, P.semcnt[hsem])

    t0 = P.op("pool", lambda e: e.memset(ONESB[:], 1.0))
    t1 = P.op("pool", lambda e: e.memset(ONESF[:], 1.0))
    t2 = P.op("pool", lambda e: e.memset(NEGPI[:], -PI))
    t3 = P.op("pool", lambda e: e.memset(IDENT[:], 0.0))
    t3 = P.op("pool", lambda e: e.memset(SMALL[:], 0.0))
    EPSR = SMALL[:, 20:21]
    EPSL = SMALL[:, 21:22]
    P.op("pool", lambda e: e.memset(EPSR, RMS_EPS), deps=[t3])
    t_eps = P.op("pool", lambda e: e.memset(EPSL, LN_EPS), deps=[t3])
    IDF = FST[0][:, 0:128]
    t4 = tt("dve", IDF, CONST[:, C_RP:C_RP + 128], CONST[:, C_RM:C_RM + 128], ALU.add, deps=[t_const])
    t_id = ts("dve", IDENT[:], IDF, 0.0, None, ALU.is_equal, deps=[t4, t3])
    t_ones = t2

    POSF = carve(RB, 4096, [128, T], F32)
    COS128 = carve(RB, 8192, [128, T], F32)
    SIN128 = carve(RB, 12288, [128, T], F32)
    COS64 = carve(RB, 16384, [128, T], F32)
    SINS64 = carve(RB, 20480, [128, T], F32)
    ANG = carve(RB, 24576, [128, T], F32)
    TU = carve(RB, 28672, [128, T], F32)
    TKI = carve(RB, 32768, [128, T], I32)
    TKF = carve(RB, 36864, [128, T], F32)
    TF = carve(RB, 40960, [128, T], F32)
    tp = P.op("dve", lambda e: e.tensor_copy(out=POSF, in_=POS), deps=[t_pos])
    C1 = 6.28125
    C2 = 2.0 * PI - 6.28125
    t_tab = None
    for (invc, TSIN, TCOS) in ((C_INV128, SIN128, COS128), (C_INV32, SINS64, COS64)):
        a1 = ts("dve", ANG, POSF, CONST[:, invc:invc + 1], None, ALU.mult, deps=[tp, t_const, t_tab])
        for (tab, off, addc) in ((TSIN, 0.5, 0.0), (TCOS, 0.75, 0.5 * PI)):
            u = ts("dve", TU, ANG, 1.0 / (2.0 * PI), off, ALU.mult, ALU.add, deps=[a1, t_tab])
            k1 = P.op("dve", lambda e: e.tensor_copy(out=TKI, in_=TU), deps=[u])
            k2 = P.op("dve", lambda e: e.tensor_copy(out=TKF, in_=TKI), deps=[k1])
            f1 = tt("dve", TF, TU, TKF, ALU.subtract, deps=[k2])
            f2 = ts("dve", TF, TF, 0.0, None, ALU.is_lt, deps=[f1])
            k3 = tt("dve", TKF, TKF, TF, ALU.subtract, deps=[f2])
            r1 = stt("dve", TU, TKF, -C1, ANG, ALU.mult, ALU.add, deps=[k3])
            r2 = stt("dve", TU, TKF, -C2, TU, ALU.mult, ALU.add, deps=[r1])
            r3 = ts("dve", TU, TU, addc, None, ALU.add, deps=[r2])
            r4 = ts("dve", TU, TU, -PI, PI, ALU.max, ALU.min, deps=[r3])
            t_tab = act(tab, TU, AF.Sin, deps=[r4])
    t_tab = ts("dve", SINS64, SINS64, CONST[:, C_SGN:C_SGN + 1], None, ALU.mult, deps=[t_tab])
    l1 = act(LG[:], DEC[:], AF.Exp, scale=-math.log(2.0), deps=[t_dec])
    t_lg = act(LG[:], LG[:], AF.Ln, bias=1.0, scale=-1.0, deps=[l1])

    P.barrier()
    if upto == 0:
        P.barrier()
        P.replay()
        return nc, es, P
    CQG = carve(RB, 28672, [128, 8, T], BF)
    CKVG = carve(RB, 45056, [128, 4, T], BF)
    RT = [carve(RB, 53248 + 2048 * i, [128, 512], F32) for i in range(6)]
    ACC = [STAT[:, 0:T], STAT[:, T:2 * T]]
    acc_tok = [[None, None], [None, None]]
    PSB = [PS[b][:].bitcast(BF) for b in range(8)]

    def tbs(tb):
        return slice(tb * 512, (tb + 1) * 512)

    def ht_rhs(tb):
        return lambda kc: HT[:, kc, tbs(tb)]

    def ev_store(func, dst_fn, bias=None, scale=1.0, extra=()):
        def mk(tb):
            def evac(ps, tok):
                i, tile, last = BSTP.get()
                t = act(tile, ps, func, bias=bias, scale=scale, deps=[tok, last] + list(extra))
                BSTP.store(i, dst_fn(tb), tile, t)
                return t
            return evac
        return mk

    def ev_norm_in(which, j, gcol):
        CG = CQG if which == 0 else CKVG

        def mk(tb):
            def evac(ps, tok):
                act(CG[:, j, tbs(tb)], ps, AF.Identity, bias=0.0, scale=VEC[:, gcol:gcol + 1], deps=[tok, t_vec])
                i, ft, last = FSTP.get()
                a2 = act(ft, ps, AF.Square, deps=[tok, last])
                accs = ACC[which][:, tbs(tb)]
                if acc_tok[which][tb] is None:
                    d = P.op("dve", lambda e: e.tensor_copy(out=accs, in_=ft), deps=[a2])
                else:
                    d = tt("dve", accs, accs, ft, ALU.add, deps=[a2, acc_tok[which][tb]])
                acc_tok[which][tb] = d
                FSTP.release(i, d)
                return a2
            return evac
        return mk

    rt_tok = {}

    def ev_rope_first(COS, SIN, s, need_d=True):
        def mk(tb):
            def evac(ps, tok):
                deps = [tok, t_tab, rt_tok.get(("o1", tb)), rt_tok.get(("o2", tb))]
                a = stt("dve", RT[0 + tb], ps, s, COS[:, tbs(tb)], ALU.mult, ALU.mult, deps=deps)
                if need_d:
                    a = stt("dve", RT[2 + tb], ps, s, SIN[:, tbs(tb)], ALU.mult, ALU.mult, deps=deps)
                rt_tok[("ad", tb)] = a
                return a
            return evac
        return mk

    def ev_rope_second(COS, SIN, s, dst1_fn, dst2_fn):
        def mk(tb):
            def evac(ps, tok):
                deps = [tok, t_tab, rt_tok.get("o12")]
                b = stt("dve", RT[4], ps, s, SIN[:, tbs(tb)], ALU.mult, ALU.mult, deps=deps)
                c = stt("dve", RT[5], ps, s, COS[:, tbs(tb)], ALU.mult, ALU.mult, deps=deps)
                i1, t1_, l1_ = BSTP.get()
                o1 = tt("dve", t1_, RT[0 + tb], RT[4], ALU.subtract, deps=[b, rt_tok[("ad", tb)], l1_])
                BSTP.store(i1, dst1_fn(tb), t1_, o1)
                i2, t2_, l2_ = BSTP.get()
                o2 = tt("dve", t2_, RT[5], RT[2 + tb], ALU.add, deps=[c, l2_])
                BSTP.store(i2, dst2_fn(tb), t2_, o2)
                rt_tok[("o1", tb)] = o1
                rt_tok[("o2", tb)] = o2
                rt_tok["o12"] = o2
                return c
            return evac
        return mk

    def job_in(j, mk):
        gemm_job(w_in[j], 32, [(512, ht_rhs(tb), mk(tb), [t_ht]) for tb in range(NTB)])

    for j in range(8):
        job_in(j, ev_norm_in(0, j, V_QG + j))
    for j in range(4):
        job_in(8 + j, ev_norm_in(1, j, V_KG + j))

    def ev_kr_first(tb):
        def evac(ps, tok):
            a = tt("dve", RT[0 + tb], ps, COS64[:, tbs(tb)], ALU.mult, deps=[tok, t_tab])
            rt_tok[("ad", tb)] = a
            return a
        return evac

    def ev_kr_second(tb):
        def evac(ps, tok):
            b = tt("dve", RT[4], ps, SINS64[:, tbs(tb)], ALU.mult, deps=[tok, t_tab, rt_tok.get("o12")])
            i, tile, last = BSTP.get()
            o = tt("dve", tile, RT[0 + tb], RT[4], ALU.add, deps=[b, rt_tok[("ad", tb)], last])
            BSTP.store(i, kv_src[2048:2176, tbs(tb)], tile, o)
            rt_tok["o12"] = o
            rt_tok[("o1", tb)] = o
            return b
        return evac

    job_in(12, ev_kr_first)
    job_in(13, ev_kr_second)

    for (base, dst, s) in ((14, rqt, 1.0), (30, rkt, 1.0 / 16.0)):
        for h in range(8):
            job_in(base + 2 * h, ev_rope_first(COS128, SIN128, s))
            job_in(base + 2 * h + 1, ev_rope_second(
                COS128, SIN128, s,
                (lambda tb, h=h, dst=dst: dst[2 * h][:, tbs(tb)]),
                (lambda tb, h=h, dst=dst: dst[2 * h + 1][:, tbs(tb)])))
    for j in range(32):
        job_in(46 + j, ev_store(AF.Copy, (lambda tb, j=j: rvt[j][:, tbs(tb)])))
    for j in range(32):
        job_in(78 + j, ev_store(AF.Silu, (lambda tb, j=j: rgt[j][:, tbs(tb)])))
    for j in range(64):
        job_in(110 + j, ev_store(AF.Sigmoid, (lambda tb, j=j: gtt[j][:, tbs(tb)]),
                                 bias=VEC[:, V_GB + j:V_GB + j + 1], extra=[t_vec]))

    rstd_tok = [[None, None], [None, None]]
    for which, n in ((0, 1024), (1, 512)):
        for tb in range(NTB):
            b = P.next_bank()
            accs = ACC[which][:, tbs(tb)]
            m = mm(PS[b][:], ONESF[:], accs, True, True, True, deps=[acc_tok[which][tb], P.bank_free[b], t1])
            r1 = act(accs, PS[b][:], AF.Sqrt, bias=EPSR, scale=1.0 / n, deps=[m, t_eps])
            r2 = P.op("dve", lambda e, accs=accs: e.reciprocal(out=accs, in_=accs), deps=[r1])
            P.bank_free[b] = r1
            rstd_tok[which][tb] = r2
    RSTD = ACC

    def cq_rhs(tb):
        return lambda kc: CQG[:, kc, tbs(tb)]

    def ev_scaled_store(which, dst_fn):
        def mk(tb):
            def evac(ps, tok):
                i, tile, last = BSTP.get()
                o = tt("dve", tile, ps, RSTD[which][:, tbs(tb)], ALU.mult, deps=[tok, last, rstd_tok[which][tb]])
                BSTP.store(i, dst_fn(tb), tile, o)
                return o
            return evac
        return mk

    def ev_qpe_first(tb):
        def evac(ps, tok):
            a = tt("dve", RT[0 + tb], ps, COS64[:, tbs(tb)], ALU.mult, deps=[tok, rt_tok.get(("o1", tb))])
            rt_tok[("ad", tb)] = a
            return a
        return evac

    def ev_qpe_second(j):
        def mk(tb):
            def evac(ps, tok):
                b = tt("dve", RT[4], ps, SINS64[:, tbs(tb)], ALU.mult, deps=[tok, rt_tok.get("o12")])
                c = tt("dve", RT[4], RT[0 + tb], RT[4], ALU.add, deps=[b, rt_tok[("ad", tb)]])
                i, tile, last = BSTP.get()
                o = tt("dve", tile, RT[4], RSTD[0][:, tbs(tb)], ALU.mult, deps=[c, last, rstd_tok[0][tb]])
                BSTP.store(i, qpt[j][:, tbs(tb)], tile, o)
                rt_tok["o12"] = o
                rt_tok[("o1", tb)] = o
                return b
            return evac
        return mk

    cq_ready = [acc_tok[0][1]]
    t_cqg_done = P.op("act", lambda e: e.activation(out=SMALL[:, 0:1], in_=SMALL[:, 1:2], func=AF.Copy))
    for h in range(16):
        gemm_job(w_uq[h], 8, [(512, cq_rhs(tb), ev_scaled_store(0, (lambda tb, h=h: qnt[h][:, tbs(tb)]))(tb),
                               [t_cqg_done]) for tb in range(NTB)])
    for j in range(8):
        gemm_job(w_uq[16 + j], 8, [(512, cq_rhs(tb), ev_qpe_first(tb), [t_cqg_done]) for tb in range(NTB)])
        gemm_job(w_uq[24 + j], 8, [(512, cq_rhs(tb), ev_qpe_second(j)(tb), [t_cqg_done]) for tb in range(NTB)])

    def ckv_rhs(tb):
        return lambda kc: CKVG[:, kc, tbs(tb)]

    for h in range(16):
        gemm_job(w_ukv[h], 4, [(512, ckv_rhs(tb),
                                ev_scaled_store(1, (lambda tb, h=h: kv_src[h * 128:(h + 1) * 128, tbs(tb)]))(tb),
                                [t_cqg_done]) for tb in range(NTB)])

    Vv = kv_src[2176:4224, :].rearrange("(t a) c -> t (a c)", a=2).rearrange("(tc p) f -> p tc f", p=128)
    VTT = TilePool(P, [carve(RA, 1024 * i, [128, 512], BF) for i in range(6)], with_sems=False)
    VOT = TilePool(P, [carve(RA, 8192 + 1024 * i, [128, 512], BF) for i in range(4)])
    t_hdead = (P.q["pe"].sem, P.semcnt[P.q["pe"].sem])

    def transpose_store(src_tile, src_tok, pool_i, srcpool, dst_ap, n=4):
        b = P.next_bank()
        t = None
        for i in range(n):
            t = tr(PSB[b][:, i * 128:(i + 1) * 128], src_tile[:, i * 128:(i + 1) * 128],
                   deps=[src_tok, P.bank_free[b], t_id], sig=(i == n - 1))
        if srcpool is not None:
            srcpool.release(pool_i, t)
        i2, ot, last = VOT.get()
        c = act(ot[:, 0:n * 128], PSB[b][:, 0:n * 128], AF.Copy, deps=[t, last])
        P.bank_free[b] = c
        VOT.store(i2, dst_ap, ot[:, 0:n * 128].rearrange("p (a b) -> p a b", a=n), c)

    def ev_vt(h):
        def mk(tb):
            def evac(ps, tok):
                i, tile, last = VTT.get()
                o = tt("dve", tile, ps, RSTD[1][:, tbs(tb)], ALU.mult, deps=[tok, last, rstd_tok[1][tb], t_hdead])
                P.deferred.append(lambda: transpose_store(
                    tile, o, i, VTT, Vv[:, tb * 4:(tb + 1) * 4, h * 128:(h + 1) * 128]))
                return o
            return evac
        return mk

    for h in range(16):
        gemm_job(w_ukv[16 + h], 4, [(512, ckv_rhs(tb), ev_vt(h)(tb), [t_cqg_done]) for tb in range(NTB)])
    flush_deferred()
    P.barrier()
    if upto == 1:
        P.barrier()
        P.replay()
        return nc, es, P
    cc_sem = P.new_sem()
    GROUPS = [[0, 1], [2, 3], [4, 5], [6, 7]]

    def allgather(src, dst):
        P.barrier(queues=("pool",))
        P.semcnt[cc_sem] += 1
        P.q["pool"].ops.append(("op", lambda e: e.collective_compute(
            "AllGather", ALU.bypass, replica_groups=GROUPS, ins=[src.opt()], outs=[dst.opt()]), cc_sem, 1))
        tok = (cc_sem, P.semcnt[cc_sem])
        for qn in ("pe", "act", "dve", "sp", "pool"):
            P.wait(qn, tok)
        return tok

    t_agkv = None
    for i, (a_, b_) in enumerate(KV_CH):
        t_agkv = allgather(kv_src[a_:b_, :], kv_dst[i])

    R_ROT = carve(RA, 49152, [128, 4, T], BF)
    R_DT = carve(RA, 57344, [128, 128], F32)
    R_XIF = carve(RA, 57856, [128, 128], F32)
    R_XIB = carve(RA, 58368, [128, 128], F32)
    R_TMP = carve(RA, 58880, [128, 128], F32)
    R_PT = [carve(RA, 59392 + 256 * i, [128, 128], BF) for i in range(2)]
    R_ON = [carve(RA, 59904 + 1024 * i, [128, 512], BF) for i in range(2)]
    R_KZF = carve(RB, 0, [128, 8, 256], BF)
    R_KZB = carve(RB, 4096, [128, 8, 256], BF)
    R_VTM = carve(RB, 8192, [128, 8, 512], BF)
    R_QXF = carve(RB, 16384, [128, 2, T], BF)
    R_QXB = carve(RB, 20480, [128, 2, T], BF)
    R_SFB = carve(RB, 24576, [128, 8, 2, 512], BF)
    R_SBB = carve(RB, 40960, [128, 8, 2, 512], BF)
    R_SF = carve(RB, 57344, [128, 2, 512], F32)
    R_SB = carve(RB, 61440, [128, 2, 512], F32)
    ZF, ZB, GF, GB = (SMALL[:, 4 + i:5 + i] for i in range(4))
    rsem = [P.new_sem() for _ in range(4)]
    rot_sem = P.new_sem()
    sfs_sem = P.new_sem()
    rstate = {"set_free": [None, None], "derived": None, "rot_store": None, "hi": 0}

    def ret_head(h, full):
        si = rstate["hi"] % 2
        rstate["hi"] += 1
        base = si * 24576
        QT = carve(RA, base, [128, 2, T], BF)
        KT = carve(RA, base + 4096, [128, 2, T], BF)
        VT = carve(RA, base + 8192, [128, 4, T], BF)
        GT = carve(RA, base + 16384, [128, 4, T], BF)
        fre = [rstate["set_free"][si]]
        lk = P.dma("sp", KT, rkt[2 * h:2 * h + 2].rearrange("c p t -> p c t"), rsem[si * 2], deps=fre)
        lv = P.dma("sp", VT, rvt[4 * h:4 * h + 4].rearrange("c p t -> p c t"), rsem[si * 2], deps=fre)
        lqg = None
        if full:
            P.dma("sp", QT, rqt[2 * h:2 * h + 2].rearrange("c p t -> p c t"), rsem[si * 2 + 1], deps=fre)
            lqg = P.dma("sp", GT, rgt[4 * h:4 * h + 4].rearrange("c p t -> p c t"), rsem[si * 2 + 1], deps=fre)
        lkv = (rsem[si * 2], P.semcnt[rsem[si * 2]])
        prevd = rstate["derived"]
        lgf = LG[:, h:h + 1]
        lgb = LG[:, 8 + h:9 + h]
        c1 = act(ZF, CONST[:, C_MC1:C_MC1 + 1], AF.Exp, scale=lgf, deps=[t_lg, t_const, prevd])
        c1 = act(ZB, CONST[:, C_MC2:C_MC2 + 1], AF.Exp, scale=lgb, deps=[c1])
        c1 = act(GF, CONST[:, C_N2:C_N2 + 1], AF.Exp, scale=lgf, deps=[c1])
        t_sc = act(GB, CONST[:, C_N2:C_N2 + 1], AF.Exp, scale=lgb, deps=[c1])
        t_tabs = None
        if full:
            d1 = ts("dve", R_TMP, CONST[:, C_RP:C_RP + 128], lgf, None, ALU.mult, deps=[prevd, t_lg])
            d2 = stt("dve", R_TMP, CONST[:, C_RM:C_RM + 128], lgb, R_TMP, ALU.mult, ALU.add, deps=[d1])
            a1 = act(R_DT, R_TMP, AF.Exp, deps=[d2])
            a2 = act(R_XIF, CONST[:, C_N1:C_N1 + 128], AF.Exp, scale=lgf, deps=[prevd])
            t_tabs = act(R_XIB, CONST[:, C_N2:C_N2 + 128], AF.Exp, scale=lgb, deps=[a2])
            qv = QT.rearrange("p a (c n) -> p (a c) n", n=128)
            qx1 = tt("dve", R_QXF.rearrange("p a (c n) -> p (a c) n", n=128), qv,
                     R_XIF.unsqueeze(1).to_broadcast([128, 16, 128]), ALU.mult, deps=[lqg, t_tabs, prevd])
            qx2 = tt("dve", R_QXB.rearrange("p a (c n) -> p (a c) n", n=128), qv,
                     R_XIB.unsqueeze(1).to_broadcast([128, 16, 128]), ALU.mult, deps=[lqg, t_tabs, prevd])
            t_qx = qx2
        t_kz = None
        for half in range(2):
            b = P.next_bank()
            t = None
            for tcl in range(4):
                tc = half * 4 + tcl
                for dc in range(2):
                    t = tr(PSB[b][:, (tcl * 2 + dc) * 128:(tcl * 2 + dc + 1) * 128], KT[:, dc, tc * 128:(tc + 1) * 128],
                           deps=[lkv, P.bank_free[b], t_id], sig=(tcl == 3 and dc == 1))
            e1 = act(R_KZF[:, half * 4:half * 4 + 4, :].rearrange("p a b -> p (a b)"), PSB[b][:, :], AF.Identity,
                     bias=0.0, scale=ZF, deps=[t, t_sc, prevd])
            e2 = ts("dve", R_KZB[:, half * 4:half * 4 + 4, :].rearrange("p a b -> p (a b)"), PSB[b][:, :], ZB, None,
                    ALU.mult, deps=[t, t_sc, prevd, e1])
            P.bank_free[b] = e2
            t_kz = (e1, e2)
        t_v = []
        for tp2 in range(4):
            b = P.next_bank()
            t = None
            for tcl in range(2):
                tc = tp2 * 2 + tcl
                for ec in range(4):
                    t = tr(PSB[b][:, (tcl * 4 + ec) * 128:(tcl * 4 + ec + 1) * 128], VT[:, ec, tc * 128:(tc + 1) * 128],
                           deps=[lkv, P.bank_free[b], t_id], sig=(tcl == 1 and ec == 3))
            dst = R_VTM[:, tp2 * 2:tp2 * 2 + 2, :].rearrange("p a b -> p (a b)")
            if tp2 % 2 == 0:
                e = act(dst, PSB[b][:, :], AF.Copy, deps=[t, prevd])
            else:
                e = P.op("dve", lambda en, dst=dst, b=b: en.tensor_copy(out=dst, in_=PSB[b][:, :]), deps=[t, prevd])
            P.bank_free[b] = e
            t_v.append(e)
        z = P.op("dve", lambda e: e.memset(R_SF.rearrange("p a b -> p (a b)"), 0.0), deps=[prevd])
        sf_tok = [z, z]
        sfb_tok = [None] * 8
        for c in range(8):
            if full:
                cp = act(R_SFB[:, c, :, :].rearrange("p a b -> p (a b)"), R_SF.rearrange("p a b -> p (a b)"), AF.Copy,
                         deps=[sf_tok[0], sf_tok[1], prevd])
                sfb_tok[c] = cp
            else:
                cp = None
            for dc in range(2):
                b = P.next_bank()
                m = mm(PS[b][:], R_KZF[:, c, dc * 128:(dc + 1) * 128], R_VTM[:, c, :], True, True, True,
                       deps=[t_kz[0], t_kz[1], t_v[c // 2], P.bank_free[b]])
                u = stt("dve", R_SF[:, dc, :], R_SF[:, dc, :], GF, PS[b][:], ALU.mult, ALU.add,
                        deps=[m, sf_tok[dc], cp, t_sc])
                P.bank_free[b] = u
                sf_tok[dc] = u
        if not full:
            st = P.dma("sp", sf_src[h * 256:(h + 1) * 256, :].rearrange("(dc p) e -> p dc e", p=128), R_SF, sfs_sem,
                       deps=[sf_tok[0], sf_tok[1]])
            rstate["derived"] = st
            rstate["set_free"][si] = (P.q["pe"].sem, P.semcnt[P.q["pe"].sem])
            return
        tl = []
        for r in range(2):
            for dc in range(2):
                i, ft, last = FSTP.get()
                row0 = r * 1024 + (h % 4) * 256 + dc * 128
                tl.append((i, ft, P.dma("sp", ft, sf_dst[h // 4][row0:row0 + 128, :], FSTP.sems[i],
                                        deps=[last, t_ags])))
        sb_tok = [None, None]
        for dc in range(2):
            (i0, f0, l0), (i1, f1, l1) = tl[dc], tl[2 + dc]
            x1 = ts("dve", R_SB[:, dc, :], f0, CONST[:, C_SEL0:C_SEL0 + 1], None, ALU.mult, deps=[l0, prevd, t_const])
            x2 = stt("dve", R_SB[:, dc, :], f1, CONST[:, C_SEL1:C_SEL1 + 1], R_SB[:, dc, :], ALU.mult, ALU.add,
                     deps=[l1, x1])
            FSTP.release(i0, x1)
            FSTP.release(i1, x2)
            sb_tok[dc] = x2
        sbb_tok = [None] * 8
        for c in range(7, -1, -1):
            cp = act(R_SBB[:, c, :, :].rearrange("p a b -> p (a b)"), R_SB.rearrange("p a b -> p (a b)"), AF.Copy,
                     deps=[sb_tok[0], sb_tok[1], prevd])
            sbb_tok[c] = cp
            if c == 0:
                break
            for dc in range(2):
                b = P.next_bank()
                m = mm(PS[b][:], R_KZB[:, c, dc * 128:(dc + 1) * 128], R_VTM[:, c, :], True, True, True,
                       deps=[t_kz[0], t_kz[1], t_v[c // 2], P.bank_free[b]])
                u = stt("dve", R_SB[:, dc, :], R_SB[:, dc, :], GB, PS[b][:], ALU.mult, ALU.add,
                        deps=[m, sb_tok[dc], cp, t_sc])
                P.bank_free[b] = u
                sb_tok[dc] = u
        rot_prev = rstate["rot_store"]
        pt_tok = [None, None]
        on_free = [None, None]
        sc = {}

        def scores(c):
            b = P.next_bank()
            cs = slice(c * 128, (c + 1) * 128)
            mm(PS[b][:, :128], KT[:, 0, cs], QT[:, 0, cs], True, False, False, deps=[lkv, lqg, P.bank_free[b]])
            m = mm(PS[b][:, :128], KT[:, 1, cs], QT[:, 1, cs], False, True, True)
            p = tt("dve", R_PT[c % 2], PS[b][:, :128], R_DT, ALU.mult, deps=[m, a1, pt_tok[c % 2]])
            P.bank_free[b] = p
            sc[c] = p

        def outp(c):
            b = P.next_bank()
            cs = slice(c * 128, (c + 1) * 128)
            ps = PS[b][:]
            mm(ps, R_PT[c % 2], R_VTM[:, c, :], True, False, False, deps=[sc[c], t_v[c // 2], P.bank_free[b]])
            pt_tok[c % 2] = mm(ps, R_QXF[:, 0, cs], R_SFB[:, c, 0, :], False, False, True, deps=[t_qx, sfb_tok[c]])
            mm(ps, R_QXF[:, 1, cs], R_SFB[:, c, 1, :], False, False, False)
            mm(ps, R_QXB[:, 0, cs], R_SBB[:, c, 0, :], False, False, False, deps=[sbb_tok[c]])
            m = mm(ps, R_QXB[:, 1, cs], R_SBB[:, c, 1, :], False, True, True)
            s6 = SMALL[:, 8:14]
            mv = SMALL[:, 16:18]
            n1 = P.op("dve", lambda e: e.bn_stats(out=s6, in_=ps), deps=[m])
            n2 = P.op("dve", lambda e: e.bn_aggr(out=mv, in_=s6), deps=[n1])
            n3a = act(SMALL[:, 17:18], SMALL[:, 17:18], AF.Sqrt, bias=EPSL, scale=1.0, deps=[n2, t_eps])
            n3 = P.op("dve", lambda e: e.reciprocal(out=SMALL[:, 17:18], in_=SMALL[:, 17:18]), deps=[n3a])
            on = ts("dve", R_ON[c % 2], ps, SMALL[:, 16:17], SMALL[:, 17:18], ALU.subtract, ALU.mult,
                    deps=[n3, on_free[c % 2]])
            P.bank_free[b] = on
            return on

        def outT_(c, on):
            b = P.next_bank()
            cs = slice(c * 128, (c + 1) * 128)
            t = None
            for ec in range(4):
                t = tr(PSB[b][:, ec * 128:(ec + 1) * 128], R_ON[c % 2][:, ec * 128:(ec + 1) * 128],
                       deps=[on, P.bank_free[b]], sig=(ec == 3))
            on_free[c % 2] = t
            g = tt("dve", R_ROT[:, :, cs], PSB[b][:, 0:512].rearrange("p (a b) -> p a b", a=4), GT[:, :, cs], ALU.mult,
                   deps=[t, lqg, rot_prev])
            P.bank_free[b] = g
            return g

        scores(0)
        pend = None
        g = None
        for c in range(8):
            if c + 1 < 8:
                scores(c + 1)
            on = outp(c)
            if pend is not None:
                g = outT_(*pend)
            pend = (c, on)
        g = outT_(*pend)
        st = P.dma("sp", rot[4 * h:4 * h + 4].rearrange("c p t -> p c t"), R_ROT, rot_sem, deps=[g])
        rstate["rot_store"] = st
        rstate["derived"] = g
        rstate["set_free"][si] = g

    for h in range(8):
        ret_head(h, False)
    P.barrier()
    t_ags = None
    for i in range(2):
        t_ags = allgather(sf_src[i * 1024:(i + 1) * 1024, :], sf_dst[i])
    if upto == 2:
        P.barrier()
        P.replay()
        return nc, es, P
    P.barrier()
    M_KPE = carve(RB, 0, [128, 2, T], BF)
    M_E = [carve(RB, 28672 + 1024 * i, [128, 512], BF) for i in range(4)]
    M_RS = [carve(RB, 32768 + 2048 * i, [128, 512], F32) for i in range(2)]
    kvd = [kv_dst[i].rearrange("(r x) t -> x r t", r=2) for i in range(3)]
    vdv = [[kv_dst[3 + i].rearrange("(r x) t -> r x t", r=2)[r].rearrange("(t a) c -> t (a c)", a=2)
            .rearrange("(tc p) f -> p tc f", p=128) for r in range(2)] for i in range(2)]
    msem = [P.new_sem() for _ in range(3)]
    t_kpe = P.dma("sp", M_KPE, kvd[2], msem[2], deps=[t_agkv])
    P.rot_banks = [0, 1, 2, 3]
    P.bank_cur = 0
    m_setfree = [None, None]
    e_free = [None] * 4
    rs_free = [None, None]
    ecur = [0]
    SCALE_A = 192.0 ** -0.5
    unit = 0
    for h in range(16):
        si = h % 2
        base = 4096 + si * 12288
        KN = carve(RB, base, [128, 2, T], BF)
        VV = carve(RB, base + 4096, [128, 2, 8, 128], BF)
        QN = carve(RB, base + 8192, [128, T], BF)
        QP = carve(RB, base + 10240, [128, T], BF)
        fre = [m_setfree[si], t_agkv]
        P.dma("sp", KN, kvd[h // 8][(h % 8) * 128:(h % 8 + 1) * 128], msem[si], deps=fre)
        for r in range(2):
            for i in range(2):
                P.dma("sp", VV[:, r, 4 * i:4 * i + 4, :], vdv[i][r][:, :, h * 128:(h + 1) * 128], msem[si], deps=fre)
        P.dma("sp", QN, qnt[h], msem[si], deps=fre)
        lm = P.dma("sp", QP, qpt[h // 2], msem[si], deps=fre)
        b0 = (h % 2) * 64
        for qb in range(2):
            ba, bs = (4, 5) if unit % 2 == 0 else (6, 7)
            unit += 1
            qs = tbs(qb)
            et_of = {}

            def S(kc):
                r, tc = kc // 8, kc % 8
                ks = slice(tc * 128, (tc + 1) * 128)
                b = P.next_bank()
                mm(PS[b][:], KN[:, r, ks], QN[:, qs], True, False, False, deps=[lm, t_kpe, P.bank_free[b]])
                m = mm(PS[b][:], M_KPE[b0:b0 + 64, r, ks], QP[b0:b0 + 64, qs], False, True, True)
                i = ecur[0] % 4
                ecur[0] += 1
                e = act(M_E[i], PS[b][:], AF.Exp, scale=SCALE_A, deps=[m, e_free[i]])
                P.bank_free[b] = e
                et_of[kc] = (i, e)

            def AV(kc):
                r, tc = kc // 8, kc % 8
                i, e = et_of[kc]
                deps = [e] + ([P.bank_free[ba], P.bank_free[bs]] if kc == 0 else [])
                mm(PS[ba][:], VV[:, r, tc, :], M_E[i], kc == 0, kc == 15, False, deps=deps)
                m2 = mm(PS[bs][:], ONESB[:], M_E[i], kc == 0, kc == 15, True)
                e_free[i] = m2
                return m2

            def later(toks):
                toks = [t for t in toks if t is not None]
                by = {}
                for t in toks:
                    by[t[0]] = max(by.get(t[0], 0), t[1])
                for s_, v_ in by.items():
                    P.wait("pe", (s_, v_))

            def Spair(kp):
                nb = [P.rot_banks[(P.bank_cur + d) % 4] for d in range(2)]
                later([P.bank_free[b_] for b_ in nb] + [lm, t_kpe])
                S(2 * kp)
                S(2 * kp + 1)

            Spair(0)
            m2 = None
            for kp in range(8):
                if kp + 1 < 8:
                    Spair(kp + 1)
                later([et_of[2 * kp][1], et_of[2 * kp + 1][1]] +
                      ([P.bank_free[ba], P.bank_free[bs]] if kp == 0 else []))
                AV(2 * kp)
                m2 = AV(2 * kp + 1)
            ri = unit % 2
            rsx = P.op("dve", lambda e, ri=ri, bs=bs: e.reciprocal(out=M_RS[ri], in_=PS[bs][:]), deps=[m2, rs_free[ri]])
            P.bank_free[bs] = rsx
            i, tile, last = BSTP.get()
            o = tt("dve", tile, PS[ba][:], M_RS[ri], ALU.mult, deps=[rsx, last])
            rs_free[ri] = o
            P.bank_free[ba] = o
            BSTP.store(i, aot[h][:, qs], tile, o)
        m_setfree[si] = (P.q["pe"].sem, P.semcnt[P.q["pe"].sem])
    P.rot_banks = list(range(8))
    P.barrier()

    if upto == 3:
        P.barrier()
        P.replay()
        return nc, es, P
    rstate["set_free"] = [None, None]
    rstate["derived"] = None
    for h in range(8):
        ret_head(h, True)
    P.barrier()

    if upto == 4:
        P.barrier()
        P.replay()
        return nc, es, P
    G_RO = carve(RA, 0, [128, 32, T], BF)
    G_AO = carve(RB, 0, [128, 16, T], BF)
    G_G = [carve(RB, 32768 + 4096 * i, [128, 2, T], BF) for i in range(2)]
    gsem = [P.new_sem() for _ in range(2)]
    lro = None
    for i in range(4):
        lro = load(G_RO[:, i * 8:(i + 1) * 8, :], rot[i * 8:(i + 1) * 8].rearrange("c p t -> p c t"), 4 + i)
    lao = load(G_AO, aot.rearrange("c p t -> p c t"), 8)
    g_act = [lao, lro] + [(ldsem[4 + i], P.semcnt[ldsem[4 + i]]) for i in range(4)]
    g_free = [None, None]
    for j in range(32):
        si = j % 2
        GG = G_G[si]
        P.dma("sp", GG[:, 0, :], gtt[j], gsem[si], deps=[g_free[si]])
        lg_ = P.dma("sp", GG[:, 1, :], gtt[32 + j], gsem[si], deps=[g_free[si]])
        hold = {}

        def ev_m(tb, GG=GG, lg_=lg_, hold=hold):
            def evac(ps, tok):
                i, ft, last = FSTP.get()
                a = tt("dve", ft, ps, GG[:, 0, tbs(tb)], ALU.mult, deps=[tok, last, lg_])
                hold[tb] = (i, ft, a)
                return a
            return evac

        def ev_r(tb, GG=GG, lg_=lg_, hold=hold, j=j, si=si):
            def evac(ps, tok):
                i, ft, last = FSTP.get()
                b = tt("dve", ft, ps, GG[:, 1, tbs(tb)], ALU.mult, deps=[tok, last, lg_])
                i0, f0, a = hold[tb]
                i2, tile, l2 = BSTP.get()
                o = tt("dve", tile, f0, ft, ALU.add, deps=[a, b, l2])
                FSTP.release(i0, o)
                FSTP.release(i, o)
                BSTP.store(i2, mixt[j][:, tbs(tb)], tile, o)
                g_free[si] = o
                return b
            return evac

        gemm_job(w_brm[j], 16, [(512, (lambda kc, tb=tb: G_AO[:, kc, tbs(tb)]), ev_m(tb), g_act) for tb in range(NTB)])
        gemm_job(w_brr[j], 32, [(512, (lambda kc, tb=tb: G_RO[:, kc, tbs(tb)]), ev_r(tb), g_act) for tb in range(NTB)])
    P.barrier()

    if upto == 5:
        P.barrier()
        P.replay()
        return nc, es, P
    ACC_S, ACC_Q = STAT[:, 0:T], STAT[:, T:2 * T]
    HALO = sb("HALO", [128, 32], F32)
    HALOB = sb("HALOB", [128, 32], BF)

    def proj_gemm(W, KC, rhs_of, act_deps, res_src, tb_list, st):
        for j in range(32):
            def mk(tb, j=j):
                def evac(ps, tok):
                    i, rt, last = FSTP.get()
                    lr = P.dma("sp", rt, res_src[j][:, tbs(tb)], FSTP.sems[i], deps=[last])
                    y = stt("dve", rt, rt, ALPHA, ps, ALU.mult, ALU.add, deps=[lr, tok])
                    i2, sq, l2 = FSTP.get()
                    a = act(sq, rt, AF.Square, deps=[y, l2])
                    k = ("s", tb)
                    if st.get(k) is None:
                        d1 = P.op("dve", lambda e: e.tensor_copy(out=ACC_S[:, tbs(tb)], in_=rt), deps=[y, st.get("free")])
                        d2 = P.op("dve", lambda e: e.tensor_copy(out=ACC_Q[:, tbs(tb)], in_=sq), deps=[a, st.get("free")])
                    else:
                        d1 = tt("dve", ACC_S[:, tbs(tb)], ACC_S[:, tbs(tb)], rt, ALU.add, deps=[y, st[k]])
                        d2 = tt("dve", ACC_Q[:, tbs(tb)], ACC_Q[:, tbs(tb)], sq, ALU.add, deps=[a, d1])
                    st[k] = d2
                    FSTP.release(i2, d2)
                    P.dma("sp", ysc[j][:, tbs(tb)], rt, FSTP.sems[i], deps=[d2])
                    FSTP.release(i, (FSTP.sems[i], P.semcnt[FSTP.sems[i]]))
                    return y
                return evac
            gemm_job(W[j], KC, [(512, rhs_of(tb), mk(tb), act_deps) for tb in tb_list])

    def ln_stats(st):
        toks = {}
        for tb in range(NTB):
            b = P.next_bank()
            m = mm(PS[b][:], ONESF[:], ACC_S[:, tbs(tb)], True, True, True, deps=[st[("s", tb)], P.bank_free[b], t1])
            mean = ts("dve", ACC_S[:, tbs(tb)], PS[b][:], 1.0 / D, None, ALU.mult, deps=[m])
            P.bank_free[b] = mean
            b2 = P.next_bank()
            m2 = mm(PS[b2][:], ONESF[:], ACC_Q[:, tbs(tb)], True, True, True, deps=[st[("s", tb)], P.bank_free[b2]])
            i, ft, last = FSTP.get()
            sqm = tt("dve", ft, ACC_S[:, tbs(tb)], ACC_S[:, tbs(tb)], ALU.mult, deps=[mean, last])
            var = stt("dve", ACC_Q[:, tbs(tb)], PS[b2][:], 1.0 / D, ft, ALU.mult, ALU.subtract, deps=[m2, sqm])
            FSTP.release(i, var)
            P.bank_free[b2] = var
            sq_ = act(ACC_Q[:, tbs(tb)], ACC_Q[:, tbs(tb)], AF.Sqrt, bias=EPSL, scale=1.0, deps=[var, t_eps])
            toks[tb] = P.op("dve", lambda e, tb=tb: e.reciprocal(out=ACC_Q[:, tbs(tb)], in_=ACC_Q[:, tbs(tb)]),
                            deps=[sq_])
        return toks

    def ln_apply(toks, lni, out_bf, out_bf_dep, out_dram, halo=False):
        P.barrier(queues=("sp",))
        last_tok = None
        items = [(j, tb) for j in range(32) for tb in range(NTB)]
        loaded = {}
        LA = 1

        def issue_load(n):
            j, tb = items[n]
            i, yt, last = FSTP.get()
            ly = P.dma("sp", yt, ysc[j][:, tbs(tb)], FSTP.sems[i], deps=[last])
            loaded[n] = (i, yt, ly)

        for n in range(LA):
            issue_load(n)
        for n, (j, tb) in enumerate(items):
            if n + LA < len(items):
                issue_load(n + LA)
            gcol = VEC[:, V_LN + lni * 64 + j:V_LN + lni * 64 + j + 1]
            bcol = VEC[:, V_LN + lni * 64 + 32 + j:V_LN + lni * 64 + 32 + j + 1]
            i, yt, ly = loaded.pop(n)
            d1 = tt("dve", yt, yt, ACC_S[:, tbs(tb)], ALU.subtract, deps=[ly, toks[tb]])
            d2 = tt("dve", yt, yt, ACC_Q[:, tbs(tb)], ALU.mult, deps=[d1])
            if out_bf is not None:
                act(out_bf[:, j, tbs(tb)], yt, AF.Identity, bias=bcol, scale=gcol, deps=[d2, out_bf_dep, t_vec])
            i2, y2, l2 = FSTP.get()
            a2 = act(y2, yt, AF.Identity, bias=bcol, scale=gcol, deps=[d2, l2, t_vec])
            FSTP.release(i, a2)
            if halo and tb == 1:
                a2 = act(HALO[:, j:j + 1], y2[:, 511:512], AF.Copy, deps=[a2])
            P.dma("sp", out_dram[j][:, tbs(tb)], y2, FSTP.sems[i2], deps=[a2])
            FSTP.release(i2, (FSTP.sems[i2], P.semcnt[FSTP.sems[i2]]))
            last_tok = a2
        return last_tok

    O_MIX = carve(RA, 0, [128, 32, T], BF)
    lmx = None
    for i in range(4):
        lmx = load(O_MIX[:, i * 8:(i + 1) * 8, :], mixt[i * 8:(i + 1) * 8].rearrange("c p t -> p c t"), 4 + i)
    o_deps = [(ldsem[4 + i], P.semcnt[ldsem[4 + i]]) for i in range(4)]
    C_MEM = carve(RB, 0, [128, 32, 256], BF)
    C_CKT = carve(RB, 16384, [128, 32, 256], BF)
    C_CV = carve(RB, 32768, [128, 2, D], BF)
    memsem = P.new_sem()
    P.barrier(queues=("pool",))
    l_mem = P.dma("pool", C_MEM, memT.rearrange("(kc p) t -> p kc t", p=128), memsem)
    st1 = {}
    proj_gemm(w_o, 32, (lambda tb: (lambda kc: O_MIX[:, kc, tbs(tb)])), o_deps, xT.rearrange("(c p) t -> c p t", p=128),
              range(NTB), st1)
    t_wo_done = (P.q["pe"].sem, P.semcnt[P.q["pe"].sem])
    toks1 = ln_stats(st1)
    for j in range(32):
        def ev_ck(ps, tok, j=j):
            return act(C_CKT[:, j, :], ps, AF.Copy, deps=[tok])
        gemm_job(w_ck[j], 32, [(256, (lambda kc: C_MEM[:, kc, :]), ev_ck, [l_mem])])
    CVT = TilePool(P, [carve(RB, 49152 + 512 * i, [128, 256], BF) for i in range(4)], with_sems=False)
    for j in range(32):
        def ev_cv(ps, tok, j=j):
            i, tile, last = CVT.get()
            c = act(tile, ps, AF.Copy, deps=[tok, last])

            def later(i=i, tile=tile, c=c, j=j):
                b = P.next_bank()
                t = None
                for mc in range(2):
                    t = tr(PSB[b][:, mc * 128:(mc + 1) * 128], tile[:, mc * 128:(mc + 1) * 128],
                           deps=[c, P.bank_free[b], t_id], sig=(mc == 1))
                CVT.release(i, t)
                e = P.op("dve", lambda en: en.tensor_copy(
                    out=C_CV[:, :, j * 128:(j + 1) * 128], in_=PSB[b][:, 0:256].rearrange("p (a b) -> p a b", a=2)),
                    deps=[t])
                P.bank_free[b] = e
            P.deferred.append(later)
            return c
        gemm_job(w_cv[j], 32, [(256, (lambda kc: C_MEM[:, kc, :]), ev_cv, [l_mem])])
    flush_deferred()
    H1 = carve(RA, 0, [128, 32, T], BF)
    t_h1 = ln_apply(toks1, 0, H1, t_wo_done, h1s)
    for j in range(32):
        gemm_job(w_cq[j], 32, [(512, (lambda kc, tb=tb: H1[:, kc, tbs(tb)]),
                                ev_store(AF.Copy, (lambda tb, j=j: cqt[j][:, tbs(tb)]))(tb), [t_h1])
                               for tb in range(NTB)])
    P.barrier()

    if upto == 6:
        P.barrier()
        P.replay()
        return nc, es, P
    X_CO = carve(RA, 0, [128, 32, T], BF)
    X_CQ = [carve(RB, 0, [128, 8, T], BF), carve(RB, 49152, [128, 8, T], BF)]
    X_E = [FST[3][:].bitcast(BF), FST[4][:].bitcast(BF)]
    X_RS = FST[2][:]
    xsem = [P.new_sem() for _ in range(2)]
    x_free = [None, None]
    xe_free = [None, None]
    xrs_free = None
    SCALE_X = 1024.0 ** -0.5
    xu = 0
    for hh in range(4):
        si = hh % 2
        lq = P.dma("sp", X_CQ[si], cqt[hh * 8:(hh + 1) * 8].rearrange("c p t -> p c t"), xsem[si], deps=[x_free[si]])
        for qb in range(2):
            qs = tbs(qb)
            ei = xu % 2
            xu += 1
            e_tok = None
            for mc in range(2):
                b = P.next_bank()
                m = None
                for dc in range(8):
                    m = mm(PS[b][:], C_CKT[:, hh * 8 + dc, mc * 128:(mc + 1) * 128], X_CQ[si][:, dc, qs],
                           dc == 0, dc == 7, dc == 7, deps=[lq, P.bank_free[b]] if dc == 0 else [])
                e_tok = act(X_E[ei][:, mc * 512:(mc + 1) * 512], PS[b][:], AF.Exp, scale=SCALE_X,
                            deps=[m, xe_free[ei]])
                P.bank_free[b] = e_tok
            b = P.next_bank()
            mm(PS[b][:], ONESB[:], X_E[ei][:, 0:512], True, False, False, deps=[e_tok, P.bank_free[b]])
            m = mm(PS[b][:], ONESB[:], X_E[ei][:, 512:1024], False, True, True)
            rsx = P.op("dve", lambda e, b=b: e.reciprocal(out=X_RS, in_=PS[b][:]), deps=[m, xrs_free])
            P.bank_free[b] = rsx
            o = None
            for dc in range(8):
                b = P.next_bank()
                col = (hh * 8 + dc) * 128
                mm(PS[b][:], C_CV[:, 0, col:col + 128], X_E[ei][:, 0:512], True, False, False, deps=[P.bank_free[b]])
                m = mm(PS[b][:], C_CV[:, 1, col:col + 128], X_E[ei][:, 512:1024], False, True, True)
                o = tt("dve", X_CO[:, hh * 8 + dc, qs], PS[b][:], X_RS, ALU.mult, deps=[m, rsx])
                P.bank_free[b] = o
            xrs_free = o
            xe_free[ei] = (P.q["pe"].sem, P.semcnt[P.q["pe"].sem])
        x_free[si] = (P.q["pe"].sem, P.semcnt[P.q["pe"].sem])
    P.barrier()

    if upto == 7:
        P.barrier()
        P.replay()
        return nc, es, P
    st2 = {}
    proj_gemm(w_co, 32, (lambda tb: (lambda kc: X_CO[:, kc, tbs(tb)])), [], h1s, range(NTB), st2)
    t_wco_done = (P.q["pe"].sem, P.semcnt[P.q["pe"].sem])
    toks2 = ln_stats(st2)
    H2 = carve(RA, 0, [128, 32, T], BF)
    ln_apply(toks2, 1, H2, t_wco_done, h2s, halo=True)
    P.barrier()

    if upto == 8:
        P.barrier()
        P.replay()
        return nc, es, P
    hsem2 = P.new_sem()
    P.dma("sp", hsrc, HALO[:], hsem2)
    P.barrier()
    t_agh = allgather(hsrc, hdst)
    HG = carve(RB, 0, [128, 2, 32], F32)
    lh = P.dma("sp", HG, hdst.rearrange("(r p) f -> p r f", r=2), hsem2, deps=[t_agh])
    hx = ts("dve", HALO[:], HG[:, 0, :], CONST[:, C_SEL0:C_SEL0 + 1], None, ALU.mult, deps=[lh])
    hx = stt("dve", HALO[:], HG[:, 1, :], CONST[:, C_SEL1:C_SEL1 + 1], HALO[:], ALU.mult, ALU.add, deps=[hx])
    t_halo = P.op("dve", lambda e: e.tensor_copy(out=HALOB[:], in_=HALO[:]), deps=[hx])
    P.barrier()

    if upto == 9:
        P.barrier()
        P.replay()
        return nc, es, P
    F_UB = [[carve(RB, 1024 + (s * 2 + w) * 4608, [128, 1026], F32) for w in range(2)] for s in range(2)]
    F_CV = [[carve(RB, 20480 + (s * 2 + w) * 4096, [128, T], F32) for w in range(2)] for s in range(2)]
    F_OUT = TilePool(P, [carve(RB, 36864 + 2048 * i, [128, T], BF) for i in range(2)])
    z0 = None
    for s_ in range(2):
        for w_ in range(2):
            z0 = P.op("dve", lambda e, s_=s_, w_=w_: e.memset(F_UB[s_][w_][:, 0:1], 0.0))
    ub_free = [[None, None], [None, None]]
    cv_free = [[None, None], [None, None]]
    P.rot_banks = [0, 1, 2, 3, 4, 5, 6]
    P.bank_cur = 0
    for jf in range(NFF):
        s_ = jf % 2
        parts = {}
        for w_ in range(2):
            cidx = 2 * jf + w_
            UBt = F_UB[s_][w_]

            def mk(tb, UBt=UBt, s_=s_, w_=w_):
                def evac(ps, tok):
                    return act(UBt[:, 1 + tb * 512:1 + (tb + 1) * 512], ps, AF.Copy, deps=[tok, ub_free[s_][w_], z0])
                return evac

            def mk_h(UBt=UBt, s_=s_, w_=w_):
                def evac(ps, tok):
                    return act(UBt[:, 1025:1026], ps, AF.Copy, deps=[tok, ub_free[s_][w_]])
                return evac
            groups = [(512, (lambda kc, tb=tb: H2[:, kc, tbs(tb)]), mk(tb), []) for tb in range(NTB)]
            s0, k, wtok, wt = load_w(w_f1[cidx], 32)
            last = None
            for (N, rhs_fn, evac, deps) in groups:
                b = P.next_bank()
                P.wait("pe", wtok)
                P.wait("pe", P.bank_free[b])
                tok = None
                for kc in range(32):
                    tok = mm(PS[b][:, :N], wt[:, kc * 128:(kc + 1) * 128], rhs_fn(kc), kc == 0, kc == 31, kc == 31)
                P.bank_free[b] = evac(PS[b][:, :N], tok)
            P.wait("pe", P.bank_free[7])
            P.wait("pe", t_halo)
            tok = None
            for kc in range(32):
                tok = mm(PS[7][:, 0:1], wt[:, kc * 128:(kc + 1) * 128], HALOB[:, kc:kc + 1], kc == 0, kc == 31, kc == 31)
            hal = mk_h()(PS[7][:, 0:1], tok)
            P.bank_free[7] = hal
            slot_free[s0] = tok
            w0 = VEC[:, V_CW + cidx:V_CW + cidx + 1]
            w1 = VEC[:, V_CW + 172 + cidx:V_CW + 172 + cidx + 1]
            w2 = VEC[:, V_CW + 344 + cidx:V_CW + 344 + cidx + 1]
            cb = VEC[:, V_CB + cidx:V_CB + cidx + 1]
            CVt = F_CV[s_][w_]
            P.wait("dve", P.bank_free[P.rot_banks[(P.bank_cur - 1) % 7]])
            P.wait("dve", P.bank_free[P.rot_banks[(P.bank_cur - 2) % 7]])
            c1 = ts("dve", CVt, UBt[:, 1:1025], w1, cb, ALU.mult, ALU.add, deps=[hal, cv_free[s_][w_], t_vec])
            c2 = stt("dve", CVt, UBt[:, 0:1024], w0, CVt, ALU.mult, ALU.add, deps=[c1])
            c3 = stt("dve", CVt, UBt[:, 2:1026], w2, CVt, ALU.mult, ALU.add, deps=[c2])
            ub_free[s_][w_] = c3
            parts[w_] = c3
        sg = act(F_CV[s_][1], F_CV[s_][1], AF.Silu, deps=[parts[1]])
        i, ot, last = F_OUT.get()
        o = tt("dve", ot, F_CV[s_][0], F_CV[s_][1], ALU.mult, deps=[sg, parts[0], last])
        cv_free[s_] = [o, o]
        F_OUT.store(i, fft[jf], ot, o)
    P.rot_banks = list(range(8))
    P.barrier()

    if upto == 10:
        P.barrier()
        P.replay()
        return nc, es, P
    F_ACT = RAB[:, 0:NFF * 512].rearrange("p (a b) -> p a b", a=NFF)
    st3 = {}
    fsem = P.new_sem()
    for tb in range(NTB):
        P.barrier()
        la = None
        for i0 in range(0, NFF, 22):
            i1 = min(NFF, i0 + 22)
            la = P.dma("sp", F_ACT[:, i0:i1, :], fft[i0:i1, :, tbs(tb)].rearrange("c p t -> p c t"), fsem)
        la = (fsem, P.semcnt[fsem])
        proj_gemm(w_f2, NFF, (lambda tb_: (lambda kc: F_ACT[:, kc, :])), [la], h2s, [tb], st3)
    toks3 = ln_stats(st3)
    ln_apply(toks3, 2, None, None, outT)
    P.barrier()
    P.replay()
    return nc, es, P


def _fm(W, cols=None):
    if cols is not None:
        W = W[:, cols]
    K, N = W.shape
    return np.ascontiguousarray(W.reshape(K // 128, 128, N // 128, 128).transpose(2, 1, 0, 3)).reshape(
        N // 128, 128, K)


_CACHE = {}


def kernel(x, mem, positions, w_in, gate_bias, q_norm_g, w_uq, kv_norm_g, w_ukv, ret_decay_fwd, ret_decay_bwd,
           w_br_mla, w_br_ret, w_o, ln1_g, ln1_b, w_cq, w_ck, w_cv, w_co, ln2_g, ln2_b, w_ffn_in, conv_w, conv_b,
           w_ffn_out, ln3_g, ln3_b):
    f32 = np.float32
    A = lambda a: np.asarray(a)
    x, mem, positions = A(x), A(mem), A(positions)
    kr = list(range(1536, 1600))
    krs = kr[32:] + kr[:32]
    cols = (list(range(0, 1536)) + kr + kr + krs + krs + list(range(1600, 22080)))
    W = {}
    W["w_in"] = _fm(A(w_in)[0], cols)
    cu = []
    for h in range(16):
        cu += list(range(h * 192, h * 192 + 128))
    for j in range(8):
        for h in (2 * j, 2 * j + 1):
            cu += list(range(h * 192 + 128, h * 192 + 192))
    for j in range(8):
        for h in (2 * j, 2 * j + 1):
            cu += list(range(h * 192 + 160, h * 192 + 192)) + list(range(h * 192 + 128, h * 192 + 160))
    W["w_uq"] = _fm(A(w_uq)[0], cu)
    ck = []
    for h in range(16):
        ck += list(range(h * 256, h * 256 + 128))
    for h in range(16):
        ck += list(range(h * 256 + 128, h * 256 + 256))
    W["w_ukv"] = _fm(A(w_ukv)[0], ck)
    W["w_brm"] = _fm(A(w_br_mla)[0])
    W["w_brr"] = _fm(A(w_br_ret)[0])
    W["w_o"] = _fm(A(w_o)[0])
    W["w_cq"] = _fm(A(w_cq)[0])
    W["w_ck"] = _fm(A(w_ck)[0])
    W["w_cv"] = _fm(A(w_cv)[0])
    W["w_co"] = _fm(A(w_co)[0])
    cf = []
    for j in range(NFF):
        cf += list(range(j * 128, (j + 1) * 128)) + list(range(DFF + j * 128, DFF + (j + 1) * 128))
    W["w_f1"] = _fm(A(w_ffn_in)[0], cf)
    W["w_f2"] = _fm(A(w_ffn_out)[0])

    def colT(v):
        v = A(v).reshape(-1)
        return v.reshape(-1, 128).T

    cw = A(conv_w)[0][:, cf]
    cb = A(conv_b)[0][cf]
    lnp = [colT(A(ln1_g)[0]), colT(A(ln1_b)[0]), colT(A(ln2_g)[0]), colT(A(ln2_b)[0]), colT(A(ln3_g)[0]),
           colT(A(ln3_b)[0])]
    p = np.arange(128, dtype=f32)
    n = np.arange(128, dtype=f32)
    const = np.zeros((128, NCONST), f32)
    const[:, C_RP:C_RP + 128] = np.maximum(n[None, :] - p[:, None], 0)
    const[:, C_RM:C_RM + 128] = np.maximum(p[:, None] - n[None, :], 0)
    const[:, C_N1:C_N1 + 128] = (n + 1)[None, :]
    const[:, C_N2:C_N2 + 128] = (128 - n)[None, :]
    const[:, C_MC1] = 127 - p
    const[:, C_MC2] = p
    const[:, C_INV128] = (f32(10000.0) ** (-(np.arange(128, dtype=f32)) / f32(128))).astype(f32)
    const[:, C_INV32] = (f32(10000.0) ** (-(np.arange(128, dtype=f32) % 32) / f32(32))).astype(f32)
    const[:, C_SGN] = np.where((np.arange(128) % 64) < 32, -1.0, 1.0)

    in_maps = []
    for c in range(8):
        b, half = c // 2, c % 2
        sl = slice(half * T, (half + 1) * T)
        xs = x[b, sl]
        ps_ = positions[b, sl]
        if half == 1:
            xs = xs[::-1]
            ps_ = ps_[::-1]
        cst = const.copy()
        cst[:, C_SEL0] = 1.0 if half == 1 else 0.0
        cst[:, C_SEL1] = 1.0 if half == 0 else 0.0
        cwc = cw if half == 0 else cw[::-1]
        vec = np.concatenate([colT(A(gate_bias)[0]), colT(A(q_norm_g)[0]), colT(A(kv_norm_g)[0])] + lnp +
                             [colT(cwc[0]), colT(cwc[1]), colT(cwc[2]), colT(cb)], axis=1).astype(f32)
        assert vec.shape == (128, NVEC), vec.shape
        df, db = A(ret_decay_fwd)[0], A(ret_decay_bwd)[0]
        dec = np.concatenate([df, db] if half == 0 else [db, df]).astype(f32)
        m = {
            "xT": np.ascontiguousarray(xs.T),
            "memT": np.ascontiguousarray(mem[b].T),
            "pos": np.ascontiguousarray(np.broadcast_to(ps_.astype(np.int32)[None, :], (128, T))),
            "const": cst,
            "vec": np.ascontiguousarray(vec),
            "decay": np.ascontiguousarray(np.broadcast_to(dec[None, :], (128, 16))),
        }
        m.update(W)
        in_maps.append(m)

    if "nc" not in _CACHE:
        _CACHE["nc"] = build(UPTO)
    nc, es, P = _CACHE["nc"]
    in_maps = [{k: v for k, v in m.items() if k in P.in_names} for m in in_maps]
    res = run_bass_kernel_spmd(nc, in_maps, core_ids=list(range(8)))
    _CACHE["res"] = res
    out = np.empty((4, 2 * T, D), f32)
    for c in range(8):
        b, half = c // 2, c % 2
        o = np.asarray(res.results[c]["outT"]).reshape(D, T).T
        if half == 1:
            o = o[::-1]
        out[b, half * T:(half + 1) * T] = o
    return out
```

```python
import math
from contextlib import ExitStack

import numpy as np
import concourse.bass as bass
import concourse.mybir as mybir
from concourse.bass_utils import run_bass_kernel_spmd

F32 = mybir.dt.float32
BF = mybir.dt.bfloat16
I32 = mybir.dt.int32
AF = mybir.ActivationFunctionType
ALU = mybir.AluOpType

D = 4096
T = 1024
NTB = 2
DFF = 11008
NFF = DFF // 128
ALPHA = 2.0 ** 0.25
LN_EPS = 1e-5
RMS_EPS = 1e-6
PI = math.pi

V_GB = 0
V_QG = 64
V_KG = 72
V_LN = 76
V_CW = 268
V_CB = 784
NVEC = 956
C_RP, C_RM, C_N1, C_N2 = 0, 128, 256, 384
C_MC1, C_MC2, C_INV128, C_INV32, C_SGN, C_SEL0, C_SEL1 = 512, 513, 514, 515, 516, 517, 518
C_T1 = 520
NCONST = 528

UPTO = 99
DEBUG = []


class Q:
    def __init__(self, name, sem):
        self.name = name
        self.sem = sem
        self.ops = []
        self.seen = {}


class Prog:
    def __init__(self, nc, es):
        self.nc = nc
        self.es = es
        self.sems = []
        self.semcnt = []
        self.q = {}
        for n in ("pe", "act", "dve", "pool", "sp"):
            self.q[n] = Q(n, self.new_sem())
        self.nbank = 8
        self.bank_free = [None] * 8
        self.bank_cur = 0
        self.rot_banks = list(range(8))
        self.deferred = []

    def new_sem(self):
        h = self.es.enter_context(self.nc.semaphore(f"s{len(self.sems)}"))
        self.sems.append(h)
        self.semcnt.append(0)
        return len(self.sems) - 1

    def wait(self, qn, tok):
        if tok is None:
            return
        q = self.q[qn]
        s, v = tok
        if qn == "pe" and s == q.sem:
            return
        if q.seen.get(s, 0) >= v:
            return
        q.seen[s] = v
        q.ops.append(("wait", s, v))

    def op(self, qn, fn, deps=(), sig=True):
        q = self.q[qn]
        for d in deps:
            self.wait(qn, d)
        tok = None
        if sig:
            self.semcnt[q.sem] += 1
            tok = (q.sem, self.semcnt[q.sem])
        q.ops.append(("op", fn, q.sem if sig else None, 1))
        return tok

    def dma(self, qn, out, in_, sem, deps=()):
        q = self.q[qn]
        for d in deps:
            self.wait(qn, d)
        self.semcnt[sem] += 16
        q.ops.append(("op", lambda e: e.dma_start(out=out, in_=in_), sem, 16))
        return (sem, self.semcnt[sem])

    def barrier(self, queues=("pe", "act", "dve", "sp")):
        for qn in queues:
            for s, c in enumerate(self.semcnt):
                if c > 0:
                    self.wait(qn, (s, c))

    def next_bank(self):
        b = self.rot_banks[self.bank_cur % len(self.rot_banks)]
        self.bank_cur += 1
        return b

    def run_deferred(self):
        d, self.deferred = self.deferred, []
        for f in d:
            f()

    def replay(self):
        nc = self.nc
        sems = self.sems

        def run(q, eng):
            for o in q.ops:
                if o[0] == "wait":
                    eng.wait_ge(sems[o[1]], o[2])
                else:
                    ins = o[1](eng)
                    if o[2] is not None:
                        ins.then_inc(sems[o[2]], o[3])

        with nc.Block() as block:
            @block.tensor
            def _(e):
                run(self.q["pe"], e)

            @block.scalar
            def _(e):
                run(self.q["act"], e)

            @block.vector
            def _(e):
                run(self.q["dve"], e)

            @block.gpsimd
            def _(e):
                run(self.q["pool"], e)

            @block.sync
            def _(e):
                run(self.q["sp"], e)


class TilePool:
    def __init__(self, P, aps, with_sems=True):
        self.P = P
        self.aps = aps
        self.sems = [P.new_sem() for _ in aps] if with_sems else None
        self.last = [None] * len(aps)
        self.held = [False] * len(aps)
        self.i = 0

    def get(self):
        i = self.i
        self.i = (i + 1) % len(self.aps)
        assert not self.held[i], "tile pool too small: tile still held"
        self.held[i] = True
        return i, self.aps[i], self.last[i]

    def release(self, i, tok):
        self.last[i] = tok
        self.held[i] = False

    def store(self, i, dram, src, dep):
        tok = self.P.dma("sp", dram, src, self.sems[i], deps=[dep])
        self.release(i, tok)
        return tok


def carve(region, off_bytes, shape, dtype):
    esz = 2 if dtype == BF else 4
    n = int(np.prod(shape[1:]))
    a = region[:, off_bytes // 2: off_bytes // 2 + n * esz // 2]
    if dtype != BF:
        a = a.bitcast(dtype)
    if len(shape) == 3:
        a = a.rearrange("p (a b) -> p a b", a=shape[1])
    elif len(shape) == 4:
        a = a.rearrange("p (a b c) -> p a b c", a=shape[1], b=shape[2])
    return a


def build(upto=99):
    nc = bass.Bass("TRN2", target_bir_lowering=False)
    es = ExitStack()
    P = Prog(nc, es)

    P.in_names = []

    def din(name, shape, dt=F32, need=0):
        if upto < need:
            return None
        P.in_names.append(name)
        return nc.dram_tensor(name, list(shape), dt, kind="ExternalInput").ap()

    def dscr(name, shape, dt):
        kind = "ExternalOutput" if name in DEBUG else "Internal"
        return nc.dram_tensor(name, list(shape), dt, kind=kind).ap()

    xT = din("xT", [D, T])
    memT = din("memT", [D, 256], need=6)
    pos_d = din("pos", [128, T], I32)
    const_d = din("const", [128, NCONST])
    vec_d = din("vec", [128, NVEC])
    dec_d = din("decay", [128, 16])
    w_in = din("w_in", [174, 128, 4096])
    w_uq = din("w_uq", [32, 128, 1024])
    w_ukv = din("w_ukv", [32, 128, 512])
    w_brm = din("w_brm", need=5, shape=[32, 128, 2048])
    w_brr = din("w_brr", need=5, shape=[32, 128, 4096])
    w_o = din("w_o", [32, 128, 4096], need=6)
    w_cq = din("w_cq", [32, 128, 4096], need=6)
    w_ck = din("w_ck", [32, 128, 4096], need=6)
    w_cv = din("w_cv", [32, 128, 4096], need=6)
    w_co = din("w_co", [32, 128, 4096], need=8)
    w_f1 = din("w_f1", [172, 128, 4096], need=10)
    w_f2 = din("w_f2", [32, 128, NFF * 128], need=11)
    outT = nc.dram_tensor("outT", [32, 128, T], F32, kind="ExternalOutput").ap()

    kv_src = dscr("kv_src", [4224, T], BF)
    KV_CH = [(0, 1024), (1024, 2048), (2048, 2176), (2176, 3200), (3200, 4224)]
    kv_dst = [dscr(f"kv_dst{i}", [2 * (b_ - a_), T], BF) for i, (a_, b_) in enumerate(KV_CH)]
    qnt = dscr("qnt", [16, 128, T], BF)
    qpt = dscr("qpt", [8, 128, T], BF)
    rqt = dscr("rqt", [16, 128, T], BF)
    rkt = dscr("rkt", [16, 128, T], BF)
    rvt = dscr("rvt", [32, 128, T], BF)
    rgt = dscr("rgt", [32, 128, T], BF)
    gtt = dscr("gtt", [64, 128, T], BF)
    sf_src = dscr("sf_src", [2048, 512], F32)
    sf_dst = [dscr(f"sf_dst{i}", [2048, 512], F32) for i in range(2)]
    aot = dscr("aot", [16, 128, T], BF)
    rot = dscr("rot", [32, 128, T], BF)
    mixt = dscr("mixt", [32, 128, T], BF)
    ysc = dscr("ysc", [32, 128, T], F32)
    h1s = dscr("h1s", [32, 128, T], F32)
    h2s = dscr("h2s", [32, 128, T], F32)
    cqt = dscr("cqt", [32, 128, T], BF)
    hsrc = dscr("hsrc", [128, 32], F32)
    hdst = dscr("hdst", [256, 32], F32)
    fft = dscr("fft", [NFF, 128, T], BF)

    def sb(name, shape, dt):
        return es.enter_context(nc.sbuf_tensor(name, list(shape), dt))

    RAB = sb("RAB", [128, 65536], BF)
    RA = RAB[:, 0:32768]
    RB = RAB[:, 32768:65536]
    RW = sb("RW", [128, 24576], BF)
    CONST = sb("CONST", [128, NCONST], F32)
    VEC = sb("VEC", [128, NVEC], F32)
    DEC = sb("DEC", [128, 16], F32)
    LG = sb("LG", [128, 16], F32)
    STAT = sb("STAT", [128, 2048], F32)
    IDENT = sb("IDENT", [128, 128], BF)
    ONESB = sb("ONESB", [128, 128], BF)
    ONESF = sb("ONESF", [128, 128], F32)
    NEGPI = sb("NEGPI", [128, 1], F32)
    SMALL = sb("SMALL", [128, 64], F32)
    FST = [sb(f"FST{i}", [128, 512], F32) for i in range(5)]
    BST = [sb(f"BST{i}", [128, 512], BF) for i in range(4)]
    PS = [es.enter_context(nc.psum_tensor(f"PS{i}", [128, 512], F32)) for i in range(8)]

    FSTP = TilePool(P, [t[:] for t in FST])
    BSTP = TilePool(P, [t[:] for t in BST])
    ldsem = [P.new_sem() for _ in range(12)]

    NSLOT, SLOT = 6, 4096
    wsem = [P.new_sem() for _ in range(NSLOT)]
    slot_free = [None] * NSLOT
    wcur = [0]

    def load_w(Wap, KC):
        k = (KC * 128 + SLOT - 1) // SLOT
        s0 = wcur[0]
        if k > 1:
            s0 = ((s0 + k - 1) // k) * k
        if s0 + k > NSLOT:
            s0 = 0
        wcur[0] = (s0 + k) % NSLOT
        deps = [slot_free[s] for s in range(s0, s0 + k)]
        dst = RW[:, s0 * SLOT: s0 * SLOT + KC * 128]
        tok = P.dma("pool", dst, Wap, wsem[s0], deps=deps)
        return s0, k, tok, dst

    dstate = {"old": []}

    def gemm_job(Wap, KC, groups):
        s0, k, wtok, wt = load_w(Wap, KC)
        last = None
        for (N, rhs_fn, evac, deps) in groups:
            b = P.next_bank()
            ps = PS[b][:, :N]
            P.wait("pe", wtok)
            P.wait("pe", P.bank_free[b])
            for d in deps:
                P.wait("pe", d)
            tok = None
            for kc in range(KC):
                lhsT = wt[:, kc * 128:(kc + 1) * 128]
                rhs = rhs_fn(kc)
                tok = P.op("pe", (lambda e, ps=ps, lhsT=lhsT, rhs=rhs, st=(kc == 0), sp=(kc == KC - 1):
                                  e.matmul(ps, lhsT, rhs, start=st, stop=sp)), sig=(kc == KC - 1))
            last = tok
            P.bank_free[b] = evac(ps, tok)
        for s in range(s0, s0 + k):
            slot_free[s] = last
        prev, dstate["old"] = dstate["old"], P.deferred
        P.deferred = []
        for f in prev:
            f()

    def flush_deferred():
        for f in dstate["old"] + P.deferred:
            f()
        dstate["old"] = []
        P.deferred = []

    def mm(ps, lhsT, rhs, st, sp, sig, deps=()):
        return P.op("pe", lambda e: e.matmul(ps, lhsT, rhs, start=st, stop=sp), deps=deps, sig=sig)

    def tr(ps, in_, deps=(), sig=True):
        return P.op("pe", lambda e: e.transpose(ps, in_, IDENT[:]), deps=deps, sig=sig)

    def act(out, in_, func, bias=None, scale=1.0, deps=()):
        if bias is None:
            return P.op("act", lambda e: e.activation(out=out, in_=in_, func=func, scale=scale), deps=deps)
        return P.op("act", lambda e: e.activation(out=out, in_=in_, func=func, bias=bias, scale=scale), deps=deps)

    def tt(q, out, in0, in1, op, deps=()):
        return P.op(q, lambda e: e.tensor_tensor(out=out, in0=in0, in1=in1, op=op), deps=deps)

    def ts(q, out, in0, s1, s2, op0, op1=None, deps=()):
        if op1 is None:
            return P.op(q, lambda e: e.tensor_scalar(out=out, in0=in0, scalar1=s1, scalar2=None, op0=op0), deps=deps)
        return P.op(q, lambda e: e.tensor_scalar(out=out, in0=in0, scalar1=s1, scalar2=s2, op0=op0, op1=op1), deps=deps)

    def stt(q, out, in0, scalar, in1, op0, op1, deps=()):
        return P.op(q, lambda e: e.scalar_tensor_tensor(out=out, in0=in0, scalar=scalar, in1=in1, op0=op0, op1=op1),
                    deps=deps)

    def load(dst, src, semi, deps=()):
        return P.dma("sp", dst, src, ldsem[semi], deps=deps)

    t_const = load(CONST[:], const_d, 0)
    t_vec = load(VEC[:], vec_d, 1)
    t_dec = load(DEC[:], dec_d, 2)
    POS = carve(RB, 0, [128, T], I32)
    t_pos = load(POS, pos_d, 3)
    HT = carve(RA, 0, [128, 32, T], BF)
    hsem = P.new_sem()
    xv = xT.rearrange("(kc p) t -> p kc t", p=128)
    t_ht = None
    for i in range(4):
        t_ht = P.dma("pool", HT[:, i * 8:(i + 1) * 8, :], xv[:, i * 8:(i + 1) * 8, :], hsem)
    t_ht = (hsem, P.semcnt[hsem])

    t0 = P.op("pool", lambda e: e.memset(ONESB[:], 1.0))
    t1 = P.op("pool", lambda e: e.memset(ONESF[:], 1.0))
    t2 = P.op("pool", lambda e: e.memset(NEGPI[:], -PI))
    t3 = P.op("pool", lambda e: e.memset(IDENT[:], 0.0))
    t3 = P.op("pool", lambda e: e.memset(SMALL[:], 0.0))
    EPSR = SMALL[:, 20:21]
    EPSL = SMALL[:, 21:22]
    P.op("pool", lambda e: e.memset(EPSR, RMS_EPS), deps=[t3])
    t_eps = P.op("pool", lambda e: e.memset(EPSL, LN_EPS), deps=[t3])
    IDF = FST[0][:, 0:128]
    t4 = tt("dve", IDF, CONST[:, C_RP:C_RP + 128], CONST[:, C_RM:C_RM + 128], ALU.add, deps=[t_const])
    t_id = ts("dve", IDENT[:], IDF, 0.0, None, ALU.is_equal, deps=[t4, t3])
    t_ones = t2

    POSF = carve(RB, 4096, [128, T], F32)
    COS128 = carve(RB, 8192, [128, T], F32)
    SIN128 = carve(RB, 12288, [128, T], F32)
    COS64 = carve(RB, 16384, [128, T], F32)
    SINS64 = carve(RB, 20480, [128, T], F32)
    ANG = carve(RB, 24576, [128, T], F32)
    TU = carve(RB, 28672, [128, T], F32)
    TKI = carve(RB, 32768, [128, T], I32)
    TKF = carve(RB, 36864, [128, T], F32)
    TF = carve(RB, 40960, [128, T], F32)
    tp = P.op("dve", lambda e: e.tensor_copy(out=POSF, in_=POS), deps=[t_pos])
    C1 = 6.28125
    C2 = 2.0 * PI - 6.28125
    t_tab = None
    for (invc, TSIN, TCOS) in ((C_INV128, SIN128, COS128), (C_INV32, SINS64, COS64)):
        a1 = ts("dve", ANG, POSF, CONST[:, invc:invc + 1], None, ALU.mult, deps=[tp, t_const, t_tab])
        for (tab, off, addc) in ((TSIN, 0.5, 0.0), (TCOS, 0.75, 0.5 * PI)):
            u = ts("dve", TU, ANG, 1.0 / (2.0 * PI), off, ALU.mult, ALU.add, deps=[a1, t_tab])
            k1 = P.op("dve", lambda e: e.tensor_copy(out=TKI, in_=TU), deps=[u])
            k2 = P.op("dve", lambda e: e.tensor_copy(out=TKF, in_=TKI), deps=[k1])
            f1 = tt("dve", TF, TU, TKF, ALU.subtract, deps=[k2])
            f2 = ts("dve", TF, TF, 0.0, None, ALU.is_lt, deps=[f1])
            k3 = tt("dve", TKF, TKF, TF, ALU.subtract, deps=[f2])
            r1 = stt("dve", TU, TKF, -C1, ANG, ALU.mult, ALU.add, deps=[k3])
            r2 = stt("dve", TU, TKF, -C2, TU, ALU.mult, ALU.add, deps=[r1])
            r3 = ts("dve", TU, TU, addc, None, ALU.add, deps=[r2])
            r4 = ts("dve", TU, TU, -PI, PI, ALU.max, ALU.min, deps=[r3])
            t_tab = act(tab, TU, AF.Sin, deps=[r4])
    t_tab = ts("dve", SINS64, SINS64, CONST[:, C_SGN:C_SGN + 1], None, ALU.mult, deps=[t_tab])
    l1 = act(LG[:], DEC[:], AF.Exp, scale=-math.log(2.0), deps=[t_dec])
    t_lg = act(LG[:], LG[:], AF.Ln, bias=1.0, scale=-1.0, deps=[l1])

    P.barrier()
    if upto == 0:
        P.barrier()
        P.replay()
        return nc, es, P
    CQG = carve(RB, 28672, [128, 8, T], BF)
    CKVG = carve(RB, 45056, [128, 4, T], BF)
    RT = [carve(RB, 53248 + 2048 * i, [128, 512], F32) for i in range(6)]
    ACC = [STAT[:, 0:T], STAT[:, T:2 * T]]
    acc_tok = [[None, None], [None, None]]
    PSB = [PS[b][:].bitcast(BF) for b in range(8)]

    def tbs(tb):
        return slice(tb * 512, (tb + 1) * 512)

    def ht_rhs(tb):
        return lambda kc: HT[:, kc, tbs(tb)]

    def ev_store(func, dst_fn, bias=None, scale=1.0, extra=()):
        def mk(tb):
            def evac(ps, tok):
                i, tile, last = BSTP.get()
                t = act(tile, ps, func, bias=bias, scale=scale, deps=[tok, last] + list(extra))
                BSTP.store(i, dst_fn(tb), tile, t)
                return t
            return evac
        return mk

    def ev_norm_in(which, j, gcol):
        CG = CQG if which == 0 else CKVG

        def mk(tb):
            def evac(ps, tok):
                act(CG[:, j, tbs(tb)], ps, AF.Identity, bias=0.0, scale=VEC[:, gcol:gcol + 1], deps=[tok, t_vec])
                i, ft, last = FSTP.get()
                a2 = act(ft, ps, AF.Square, deps=[tok, last])
                accs = ACC[which][:, tbs(tb)]
                if acc_tok[which][tb] is None:
                    d = P.op("dve", lambda e: e.tensor_copy(out=accs, in_=ft), deps=[a2])
                else:
                    d = tt("dve", accs, accs, ft, ALU.add, deps=[a2, acc_tok[which][tb]])
                acc_tok[which][tb] = d
                FSTP.release(i, d)
                return a2
            return evac
        return mk

    rt_tok = {}

    def ev_rope_first(COS, SIN, s, need_d=True):
        def mk(tb):
            def evac(ps, tok):
                deps = [tok, t_tab, rt_tok.get(("o1", tb)), rt_tok.get(("o2", tb))]
                a = stt("dve", RT[0 + tb], ps, s, COS[:, tbs(tb)], ALU.mult, ALU.mult, deps=deps)
                if need_d:
                    a = stt("dve", RT[2 + tb], ps, s, SIN[:, tbs(tb)], ALU.mult, ALU.mult, deps=deps)
                rt_tok[("ad", tb)] = a
                return a
            return evac
        return mk

    def ev_rope_second(COS, SIN, s, dst1_fn, dst2_fn):
        def mk(tb):
            def evac(ps, tok):
                deps = [tok, t_tab, rt_tok.get("o12")]
                b = stt("dve", RT[4], ps, s, SIN[:, tbs(tb)], ALU.mult, ALU.mult, deps=deps)
                c = stt("dve", RT[5], ps, s, COS[:, tbs(tb)], ALU.mult, ALU.mult, deps=deps)
                i1, t1_, l1_ = BSTP.get()
                o1 = tt("dve", t1_, RT[0 + tb], RT[4], ALU.subtract, deps=[b, rt_tok[("ad", tb)], l1_])
                BSTP.store(i1, dst1_fn(tb), t1_, o1)
                i2, t2_, l2_ = BSTP.get()
                o2 = tt("dve", t2_, RT[5], RT[2 + tb], ALU.add, deps=[c, l2_])
                BSTP.store(i2, dst2_fn(tb), t2_, o2)
                rt_tok[("o1", tb)] = o1
                rt_tok[("o2", tb)] = o2
                rt_tok["o12"] = o2
                return c
            return evac
        return mk

    def job_in(j, mk):
        gemm_job(w_in[j], 32, [(512, ht_rhs(tb), mk(tb), [t_ht]) for tb in range(NTB)])

    for j in range(8):
        job_in(j, ev_norm_in(0, j, V_QG + j))
    for j in range(4):
        job_in(8 + j, ev_norm_in(1, j, V_KG + j))

    def ev_kr_first(tb):
        def evac(ps, tok):
            a = tt("dve", RT[0 + tb], ps, COS64[:, tbs(tb)], ALU.mult, deps=[tok, t_tab])
            rt_tok[("ad", tb)] = a
            return a
        return evac

    def ev_kr_second(tb):
        def evac(ps, tok):
            b = tt("dve", RT[4], ps, SINS64[:, tbs(tb)], ALU.mult, deps=[tok, t_tab, rt_tok.get("o12")])
            i, tile, last = BSTP.get()
            o = tt("dve", tile, RT[0 + tb], RT[4], ALU.add, deps=[b, rt_tok[("ad", tb)], last])
            BSTP.store(i, kv_src[2048:2176, tbs(tb)], tile, o)
            rt_tok["o12"] = o
            rt_tok[("o1", tb)] = o
            return b
        return evac

    job_in(12, ev_kr_first)
    job_in(13, ev_kr_second)

    for (base, dst, s) in ((14, rqt, 1.0), (30, rkt, 1.0 / 16.0)):
        for h in range(8):
            job_in(base + 2 * h, ev_rope_first(COS128, SIN128, s))
            job_in(base + 2 * h + 1, ev_rope_second(
                COS128, SIN128, s,
                (lambda tb, h=h, dst=dst: dst[2 * h][:, tbs(tb)]),
                (lambda tb, h=h, dst=dst: dst[2 * h + 1][:, tbs(tb)])))
    for j in range(32):
        job_in(46 + j, ev_store(AF.Copy, (lambda tb, j=j: rvt[j][:, tbs(tb)])))
    for j in range(32):
        job_in(78 + j, ev_store(AF.Silu, (lambda tb, j=j: rgt[j][:, tbs(tb)])))
    for j in range(64):
        job_in(110 + j, ev_store(AF.Sigmoid, (lambda tb, j=j: gtt[j][:, tbs(tb)]),
                                 bias=VEC[:, V_GB + j:V_GB + j + 1], extra=[t_vec]))

    rstd_tok = [[None, None], [None, None]]
    for which, n in ((0, 1024), (1, 512)):
        for tb in range(NTB):
            b = P.next_bank()
            accs = ACC[which][:, tbs(tb)]
            m = mm(PS[b][:], ONESF[:], accs, True, True, True, deps=[acc_tok[which][tb], P.bank_free[b], t1])
            r1 = act(accs, PS[b][:], AF.Sqrt, bias=EPSR, scale=1.0 / n, deps=[m, t_eps])
            r2 = P.op("dve", lambda e, accs=accs: e.reciprocal(out=accs, in_=accs), deps=[r1])
            P.bank_free[b] = r1
            rstd_tok[which][tb] = r2
    RSTD = ACC

    def cq_rhs(tb):
        return lambda kc: CQG[:, kc, tbs(tb)]

    def ev_scaled_store(which, dst_fn):
        def mk(tb):
            def evac(ps, tok):
                i, tile, last = BSTP.get()
                o = tt("dve", tile, ps, RSTD[which][:, tbs(tb)], ALU.mult, deps=[tok, last, rstd_tok[which][tb]])
                BSTP.store(i, dst_fn(tb), tile, o)
                return o
            return evac
        return mk

    def ev_qpe_first(tb):
        def evac(ps, tok):
            a = tt("dve", RT[0 + tb], ps, COS64[:, tbs(tb)], ALU.mult, deps=[tok, rt_tok.get(("o1", tb))])
            rt_tok[("ad", tb)] = a
            return a
        return evac

    def ev_qpe_second(j):
        def mk(tb):
            def evac(ps, tok):
                b = tt("dve", RT[4], ps, SINS64[:, tbs(tb)], ALU.mult, deps=[tok, rt_tok.get("o12")])
                c = tt("dve", RT[4], RT[0 + tb], RT[4], ALU.add, deps=[b, rt_tok[("ad", tb)]])
                i, tile, last = BSTP.get()
                o = tt("dve", tile, RT[4], RSTD[0][:, tbs(tb)], ALU.mult, deps=[c, last, rstd_tok[0][tb]])
                BSTP.store(i, qpt[j][:, tbs(tb)], tile, o)
                rt_tok["o12"] = o
                rt_tok[("o1", tb)] = o
                return b
            return evac
        return mk

    cq_ready = [acc_tok[0][1]]
    t_cqg_done = P.op("act", lambda e: e.activation(out=SMALL[:, 0:1], in_=SMALL[:, 1:2], func=AF.Copy))
    for h in range(16):
        gemm_job(w_uq[h], 8, [(512, cq_rhs(tb), ev_scaled_store(0, (lambda tb, h=h: qnt[h][:, tbs(tb)]))(tb),
                               [t_cqg_done]) for tb in range(NTB)])
    for j in range(8):
        gemm_job(w_uq[16 + j], 8, [(512, cq_rhs(tb), ev_qpe_first(tb), [t_cqg_done]) for tb in range(NTB)])
        gemm_job(w_uq[24 + j], 8, [(512, cq_rhs(tb), ev_qpe_second(j)(tb), [t_cqg_done]) for tb in range(NTB)])

    def ckv_rhs(tb):
        return lambda kc: CKVG[:, kc, tbs(tb)]

    for h in range(16):
        gemm_job(w_ukv[h], 4, [(512, ckv_rhs(tb),
                                ev_scaled_store(1, (lambda tb, h=h: kv_src[h * 128:(h + 1) * 128, tbs(tb)]))(tb),
                                [t_cqg_done]) for tb in range(NTB)])

    Vv = kv_src[2176:4224, :].rearrange("(t a) c -> t (a c)", a=2).rearrange("(tc p) f -> p tc f", p=128)
    VTT = TilePool(P, [carve(RA, 1024 * i, [128, 512], BF) for i in range(6)], with_sems=False)
    VOT = TilePool(P, [carve(RA, 8192 + 1024 * i, [128, 512], BF) for i in range(4)])
    t_hdead = (P.q["pe"].sem, P.semcnt[P.q["pe"].sem])

    def transpose_store(src_tile, src_tok, pool_i, srcpool, dst_ap, n=4):
        b = P.next_bank()
        t = None
        for i in range(n):
            t = tr(PSB[b][:, i * 128:(i + 1) * 128], src_tile[:, i * 128:(i + 1) * 128],
                   deps=[src_tok, P.bank_free[b], t_id], sig=(i == n - 1))
        if srcpool is not None:
            srcpool.release(pool_i, t)
        i2, ot, last = VOT.get()
        c = act(ot[:, 0:n * 128], PSB[b][:, 0:n * 128], AF.Copy, deps=[t, last])
        P.bank_free[b] = c
        VOT.store(i2, dst_ap, ot[:, 0:n * 128].rearrange("p (a b) -> p a b", a=n), c)

    def ev_vt(h):
        def mk(tb):
            def evac(ps, tok):
                i, tile, last = VTT.get()
                o = tt("dve", tile, ps, RSTD[1][:, tbs(tb)], ALU.mult, deps=[tok, last, rstd_tok[1][tb], t_hdead])
                P.deferred.append(lambda: transpose_store(
                    tile, o, i, VTT, Vv[:, tb * 4:(tb + 1) * 4, h * 128:(h + 1) * 128]))
                return o
            return evac
        return mk

    for h in range(16):
        gemm_job(w_ukv[16 + h], 4, [(512, ckv_rhs(tb), ev_vt(h)(tb), [t_cqg_done]) for tb in range(NTB)])
    flush_deferred()
    P.barrier()
    if upto == 1:
        P.barrier()
        P.replay()
        return nc, es, P
    cc_sem = P.new_sem()
    GROUPS = [[0, 1], [2, 3], [4, 5], [6, 7]]

    def allgather(src, dst):
        P.barrier(queues=("pool",))
        P.semcnt[cc_sem] += 1
        P.q["pool"].ops.append(("op", lambda e: e.collective_compute(
            "AllGather", ALU.bypass, replica_groups=GROUPS, ins=[src.opt()], outs=[dst.opt()]), cc_sem, 1))
        tok = (cc_sem, P.semcnt[cc_sem])
        for qn in ("pe", "act", "dve", "sp", "pool"):
            P.wait(qn, tok)
        return tok

    t_agkv = None
    for i, (a_, b_) in enumerate(KV_CH):
        t_agkv = allgather(kv_src[a_:b_, :], kv_dst[i])

    R_ROT = carve(RA, 49152, [128, 4, T], BF)
    R_DT = carve(RA, 57344, [128, 128], F32)
    R_XIF = carve(RA, 57856, [128, 128], F32)
    R_XIB = carve(RA, 58368, [128, 128], F32)
    R_TMP = carve(RA, 58880, [128, 128], F32)
    R_PT = [carve(RA, 59392 + 256 * i, [128, 128], BF) for i in range(2)]
    R_ON = [carve(RA, 59904 + 1024 * i, [128, 512], BF) for i in range(2)]
    R_KZF = carve(RB, 0, [128, 8, 256], BF)
    R_KZB = carve(RB, 4096, [128, 8, 256], BF)
    R_VTM = carve(RB, 8192, [128, 8, 512], BF)
    R_QXF = carve(RB, 16384, [128, 2, T], BF)
    R_QXB = carve(RB, 20480, [128, 2, T], BF)
    R_SFB = carve(RB, 24576, [128, 8, 2, 512], BF)
    R_SBB = carve(RB, 40960, [128, 8, 2, 512], BF)
    R_SF = carve(RB, 57344, [128, 2, 512], F32)
    R_SB = carve(RB, 61440, [128, 2, 512], F32)
    ZF, ZB, GF, GB = (SMALL[:, 4 + i:5 + i] for i in range(4))
    rsem = [P.new_sem() for _ in range(4)]
    rot_sem = P.new_sem()
    sfs_sem = P.new_sem()
    rstate = {"set_free": [None, None], "derived": None, "rot_store": None, "hi": 0}

    def ret_head(h, full):
        si = rstate["hi"] % 2
        rstate["hi"] += 1
        base = si * 24576
        QT = carve(RA, base, [128, 2, T], BF)
        KT = carve(RA, base + 4096, [128, 2, T], BF)
        VT = carve(RA, base + 8192, [128, 4, T], BF)
        GT = carve(RA, base + 16384, [128, 4, T], BF)
        fre = [rstate["set_free"][si]]
        lk = P.dma("sp", KT, rkt[2 * h:2 * h + 2].rearrange("c p t -> p c t"), rsem[si * 2], deps=fre)
        lv = P.dma("sp", VT, rvt[4 * h:4 * h + 4].rearrange("c p t -> p c t"), rsem[si * 2], deps=fre)
        lqg = None
        if full:
            P.dma("sp", QT, rqt[2 * h:2 * h + 2].rearrange("c p t -> p c t"), rsem[si * 2 + 1], deps=fre)
            lqg = P.dma("sp", GT, rgt[4 * h:4 * h + 4].rearrange("c p t -> p c t"), rsem[si * 2 + 1], deps=fre)
        lkv = (rsem[si * 2], P.semcnt[rsem[si * 2]])
        prevd = rstate["derived"]
        lgf = LG[:, h:h + 1]
        lgb = LG[:, 8 + h:9 + h]
        c1 = act(ZF, CONST[:, C_MC1:C_MC1 + 1], AF.Exp, scale=lgf, deps=[t_lg, t_const, prevd])
        c1 = act(ZB, CONST[:, C_MC2:C_MC2 + 1], AF.Exp, scale=lgb, deps=[c1])
        c1 = act(GF, CONST[:, C_N2:C_N2 + 1], AF.Exp, scale=lgf, deps=[c1])
        t_sc = act(GB, CONST[:, C_N2:C_N2 + 1], AF.Exp, scale=lgb, deps=[c1])
        t_tabs = None
        ZFC = SMALL[:, 24:32]
        t_zfc = None
        if not full:
            t_zfc = act(ZFC, CONST[:, C_T1:C_T1 + 8], AF.Exp, scale=lgf, deps=[t_lg, t_const, prevd])
        if full:
            d1 = ts("dve", R_TMP, CONST[:, C_RP:C_RP + 128], lgf, None, ALU.mult, deps=[prevd, t_lg])
            d2 = stt("dve", R_TMP, CONST[:, C_RM:C_RM + 128], lgb, R_TMP, ALU.mult, ALU.add, deps=[d1])
            a1 = act(R_DT, R_TMP, AF.Exp, deps=[d2])
            a2 = act(R_XIF, CONST[:, C_N1:C_N1 + 128], AF.Exp, scale=lgf, deps=[prevd])
            t_tabs = act(R_XIB, CONST[:, C_N2:C_N2 + 128], AF.Exp, scale=lgb, deps=[a2])
            qv = QT.rearrange("p a (c n) -> p (a c) n", n=128)
            qx1 = tt("dve", R_QXF.rearrange("p a (c n) -> p (a c) n", n=128), qv,
                     R_XIF.unsqueeze(1).to_broadcast([128, 16, 128]), ALU.mult, deps=[lqg, t_tabs, prevd])
            qx2 = tt("dve", R_QXB.rearrange("p a (c n) -> p (a c) n", n=128), qv,
                     R_XIB.unsqueeze(1).to_broadcast([128, 16, 128]), ALU.mult, deps=[lqg, t_tabs, prevd])
            t_qx = qx2
        t_kz = None
        for half in range(2):
            b = P.next_bank()
            t = None
            for tcl in range(4):
                tc = half * 4 + tcl
                for dc in range(2):
                    t = tr(PSB[b][:, (tcl * 2 + dc) * 128:(tcl * 2 + dc + 1) * 128], KT[:, dc, tc * 128:(tc + 1) * 128],
                           deps=[lkv, P.bank_free[b], t_id], sig=(tcl == 3 and dc == 1))
            if not full:
                e2 = tt("dve", R_KZF[:, half * 4:half * 4 + 4, :],
                        PSB[b][:, :].rearrange("p (a b) -> p a b", a=4),
                        ZFC[:, half * 4:half * 4 + 4].unsqueeze(2).to_broadcast([128, 4, 256]), ALU.mult,
                        deps=[t, t_zfc, prevd])
                e1 = e2
            else:
                e1 = act(R_KZF[:, half * 4:half * 4 + 4, :].rearrange("p a b -> p (a b)"), PSB[b][:, :], AF.Identity,
                         bias=0.0, scale=ZF, deps=[t, t_sc, prevd])
                e2 = ts("dve", R_KZB[:, half * 4:half * 4 + 4, :].rearrange("p a b -> p (a b)"), PSB[b][:, :], ZB,
                        None, ALU.mult, deps=[t, t_sc, prevd, e1])
            P.bank_free[b] = e2
            t_kz = (e1, e2)
        t_v = []
        for tp2 in range(4):
            b = P.next_bank()
            t = None
            for tcl in range(2):
                tc = tp2 * 2 + tcl
                for ec in range(4):
                    t = tr(PSB[b][:, (tcl * 4 + ec) * 128:(tcl * 4 + ec + 1) * 128], VT[:, ec, tc * 128:(tc + 1) * 128],
                           deps=[lkv, P.bank_free[b], t_id], sig=(tcl == 1 and ec == 3))
            dst = R_VTM[:, tp2 * 2:tp2 * 2 + 2, :].rearrange("p a b -> p (a b)")
            if tp2 % 2 == 0:
                e = act(dst, PSB[b][:, :], AF.Copy, deps=[t, prevd])
            else:
                e = P.op("dve", lambda en, dst=dst, b=b: en.tensor_copy(out=dst, in_=PSB[b][:, :]), deps=[t, prevd])
            P.bank_free[b] = e
            t_v.append(e)
        if not full:
            ev = []
            for dc in range(2):
                b = P.next_bank()
                m = None
                for c in range(8):
                    m = mm(PS[b][:], R_KZF[:, c, dc * 128:(dc + 1) * 128], R_VTM[:, c, :], c == 0, c == 7, c == 7,
                           deps=[t_kz[0], t_kz[1], t_v[c // 2]] + ([P.bank_free[b]] if c == 0 else []))
                if dc == 0:
                    e = act(R_SF[:, dc, :], PS[b][:], AF.Copy, deps=[m, prevd])
                else:
                    e = P.op("dve", lambda en, b=b, dc=dc: en.tensor_copy(out=R_SF[:, dc, :], in_=PS[b][:]),
                             deps=[m, prevd])
                P.bank_free[b] = e
                ev.append(e)
            st = P.dma("sp", sf_src[h * 256:(h + 1) * 256, :].rearrange("(dc p) e -> p dc e", p=128), R_SF, sfs_sem,
                       deps=ev)
            rstate["derived"] = st
            rstate["set_free"][si] = (P.q["pe"].sem, P.semcnt[P.q["pe"].sem])
            return
        z = P.op("dve", lambda e: e.memset(R_SF.rearrange("p a b -> p (a b)"), 0.0), deps=[prevd])
        sf_tok = [z, z]
        sfb_tok = [None] * 8
        for c in range(8):
            if full:
                cp = act(R_SFB[:, c, :, :].rearrange("p a b -> p (a b)"), R_SF.rearrange("p a b -> p (a b)"), AF.Copy,
                         deps=[sf_tok[0], sf_tok[1], prevd])
                sfb_tok[c] = cp
            else:
                cp = None
            for dc in range(2):
                b = P.next_bank()
                m = mm(PS[b][:], R_KZF[:, c, dc * 128:(dc + 1) * 128], R_VTM[:, c, :], True, True, True,
                       deps=[t_kz[0], t_kz[1], t_v[c // 2], P.bank_free[b]])
                u = stt("dve", R_SF[:, dc, :], R_SF[:, dc, :], GF, PS[b][:], ALU.mult, ALU.add,
                        deps=[m, sf_tok[dc], cp, t_sc])
                P.bank_free[b] = u
                sf_tok[dc] = u
        if not full:
            st = P.dma("sp", sf_src[h * 256:(h + 1) * 256, :].rearrange("(dc p) e -> p dc e", p=128), R_SF, sfs_sem,
                       deps=[sf_tok[0], sf_tok[1]])
            rstate["derived"] = st
            rstate["set_free"][si] = (P.q["pe"].sem, P.semcnt[P.q["pe"].sem])
            return
        tl = []
        for r in range(2):
            for dc in range(2):
                i, ft, last = FSTP.get()
                row0 = r * 1024 + (h % 4) * 256 + dc * 128
                tl.append((i, ft, P.dma("sp", ft, sf_dst[h // 4][row0:row0 + 128, :], FSTP.sems[i],
                                        deps=[last, t_ags])))
        sb_tok = [None, None]
        for dc in range(2):
            (i0, f0, l0), (i1, f1, l1) = tl[dc], tl[2 + dc]
            x1 = ts("dve", R_SB[:, dc, :], f0, CONST[:, C_SEL0:C_SEL0 + 1], None, ALU.mult, deps=[l0, prevd, t_const])
            x2 = stt("dve", R_SB[:, dc, :], f1, CONST[:, C_SEL1:C_SEL1 + 1], R_SB[:, dc, :], ALU.mult, ALU.add,
                     deps=[l1, x1])
            FSTP.release(i0, x1)
            FSTP.release(i1, x2)
            sb_tok[dc] = x2
        sbb_tok = [None] * 8
        for c in range(7, -1, -1):
            cp = act(R_SBB[:, c, :, :].rearrange("p a b -> p (a b)"), R_SB.rearrange("p a b -> p (a b)"), AF.Copy,
                     deps=[sb_tok[0], sb_tok[1], prevd])
            sbb_tok[c] = cp
            if c == 0:
                break
            for dc in range(2):
                b = P.next_bank()
                m = mm(PS[b][:], R_KZB[:, c, dc * 128:(dc + 1) * 128], R_VTM[:, c, :], True, True, True,
                       deps=[t_kz[0], t_kz[1], t_v[c // 2], P.bank_free[b]])
                u = stt("dve", R_SB[:, dc, :], R_SB[:, dc, :], GB, PS[b][:], ALU.mult, ALU.add,
                        deps=[m, sb_tok[dc], cp, t_sc])
                P.bank_free[b] = u
                sb_tok[dc] = u
        rot_prev = rstate["rot_store"]
        pt_tok = [None, None]
        on_free = [None, None]
        sc = {}

        def scores(c):
            b = P.next_bank()
            cs = slice(c * 128, (c + 1) * 128)
            mm(PS[b][:, :128], KT[:, 0, cs], QT[:, 0, cs], True, False, False, deps=[lkv, lqg, P.bank_free[b]])
            m = mm(PS[b][:, :128], KT[:, 1, cs], QT[:, 1, cs], False, True, True)
            p = tt("dve", R_PT[c % 2], PS[b][:, :128], R_DT, ALU.mult, deps=[m, a1, pt_tok[c % 2]])
            P.bank_free[b] = p
            sc[c] = p

        def outp(c):
            b = P.next_bank()
            cs = slice(c * 128, (c + 1) * 128)
            ps = PS[b][:]
            mm(ps, R_PT[c % 2], R_VTM[:, c, :], True, False, False, deps=[sc[c], t_v[c // 2], P.bank_free[b]])
            pt_tok[c % 2] = mm(ps, R_QXF[:, 0, cs], R_SFB[:, c, 0, :], False, False, True, deps=[t_qx, sfb_tok[c]])
            mm(ps, R_QXF[:, 1, cs], R_SFB[:, c, 1, :], False, False, False)
            mm(ps, R_QXB[:, 0, cs], R_SBB[:, c, 0, :], False, False, False, deps=[sbb_tok[c]])
            m = mm(ps, R_QXB[:, 1, cs], R_SBB[:, c, 1, :], False, True, True)
            s6 = SMALL[:, 8:14]
            mv = SMALL[:, 16:18]
            n1 = P.op("dve", lambda e: e.bn_stats(out=s6, in_=ps), deps=[m])
            n2 = P.op("dve", lambda e: e.bn_aggr(out=mv, in_=s6), deps=[n1])
            n3a = act(SMALL[:, 17:18], SMALL[:, 17:18], AF.Sqrt, bias=EPSL, scale=1.0, deps=[n2, t_eps])
            n3 = P.op("dve", lambda e: e.reciprocal(out=SMALL[:, 17:18], in_=SMALL[:, 17:18]), deps=[n3a])
            on = ts("dve", R_ON[c % 2], ps, SMALL[:, 16:17], SMALL[:, 17:18], ALU.subtract, ALU.mult,
                    deps=[n3, on_free[c % 2]])
            P.bank_free[b] = on
            return on

        def outT_(c, on):
            b = P.next_bank()
            cs = slice(c * 128, (c + 1) * 128)
            t = None
            for ec in range(4):
                t = tr(PSB[b][:, ec * 128:(ec + 1) * 128], R_ON[c % 2][:, ec * 128:(ec + 1) * 128],
                       deps=[on, P.bank_free[b]], sig=(ec == 3))
            on_free[c % 2] = t
            g = tt("dve", R_ROT[:, :, cs], PSB[b][:, 0:512].rearrange("p (a b) -> p a b", a=4), GT[:, :, cs], ALU.mult,
                   deps=[t, lqg, rot_prev])
            P.bank_free[b] = g
            return g

        scores(0)
        pend = None
        g = None
        for c in range(8):
            if c + 1 < 8:
                scores(c + 1)
            on = outp(c)
            if pend is not None:
                g = outT_(*pend)
            pend = (c, on)
        g = outT_(*pend)
        st = P.dma("sp", rot[4 * h:4 * h + 4].rearrange("c p t -> p c t"), R_ROT, rot_sem, deps=[g])
        rstate["rot_store"] = st
        rstate["derived"] = g
        rstate["set_free"][si] = g

    for h in range(8):
        ret_head(h, False)
    P.barrier()
    t_ags = None
    for i in range(2):
        t_ags = allgather(sf_src[i * 1024:(i + 1) * 1024, :], sf_dst[i])
    if upto == 2:
        P.barrier()
        P.replay()
        return nc, es, P
    P.barrier()
    M_KPE = carve(RB, 0, [128, 2, T], BF)
    M_E = [carve(RB, 28672 + 1024 * i, [128, 512], BF) for i in range(4)]
    M_RS = [carve(RB, 32768 + 2048 * i, [128, 512], F32) for i in range(2)]
    kvd = [kv_dst[i].rearrange("(r x) t -> x r t", r=2) for i in range(3)]
    vdv = [[kv_dst[3 + i].rearrange("(r x) t -> r x t", r=2)[r].rearrange("(t a) c -> t (a c)", a=2)
            .rearrange("(tc p) f -> p tc f", p=128) for r in range(2)] for i in range(2)]
    msem = [P.new_sem() for _ in range(3)]
    t_kpe = P.dma("sp", M_KPE, kvd[2], msem[2], deps=[t_agkv])
    P.rot_banks = [0, 1, 2, 3]
    P.bank_cur = 0
    m_setfree = [None, None]
    e_free = [None] * 4
    rs_free = [None, None]
    ecur = [0]
    SCALE_A = 192.0 ** -0.5
    unit = 0
    for h in range(16):
        si = h % 2
        base = 4096 + si * 12288
        KN = carve(RB, base, [128, 2, T], BF)
        VV = carve(RB, base + 4096, [128, 2, 8, 128], BF)
        QN = carve(RB, base + 8192, [128, T], BF)
        QP = carve(RB, base + 10240, [128, T], BF)
        fre = [m_setfree[si], t_agkv]
        P.dma("sp", KN, kvd[h // 8][(h % 8) * 128:(h % 8 + 1) * 128], msem[si], deps=fre)
        for r in range(2):
            for i in range(2):
                P.dma("sp", VV[:, r, 4 * i:4 * i + 4, :], vdv[i][r][:, :, h * 128:(h + 1) * 128], msem[si], deps=fre)
        P.dma("sp", QN, qnt[h], msem[si], deps=fre)
        lm = P.dma("sp", QP, qpt[h // 2], msem[si], deps=fre)
        b0 = (h % 2) * 64
        for qb in range(2):
            ba, bs = (4, 5) if unit % 2 == 0 else (6, 7)
            unit += 1
            qs = tbs(qb)
            et_of = {}

            def S(kc):
                r, tc = kc // 8, kc % 8
                ks = slice(tc * 128, (tc + 1) * 128)
                b = P.next_bank()
                mm(PS[b][:], KN[:, r, ks], QN[:, qs], True, False, False, deps=[lm, t_kpe, P.bank_free[b]])
                m = mm(PS[b][:], M_KPE[b0:b0 + 64, r, ks], QP[b0:b0 + 64, qs], False, True, True)
                i = ecur[0] % 4
                ecur[0] += 1
                e = act(M_E[i], PS[b][:], AF.Exp, scale=SCALE_A, deps=[m, e_free[i]])
                P.bank_free[b] = e
                et_of[kc] = (i, e)

            def AV(kc):
                r, tc = kc // 8, kc % 8
                i, e = et_of[kc]
                deps = [e] + ([P.bank_free[ba], P.bank_free[bs]] if kc == 0 else [])
                mm(PS[ba][:], VV[:, r, tc, :], M_E[i], kc == 0, kc == 15, False, deps=deps)
                m2 = mm(PS[bs][:], ONESB[:], M_E[i], kc == 0, kc == 15, True)
                e_free[i] = m2
                return m2

            S(0)
            S(1)
            m2 = None
            for kc in range(16):
                m2 = AV(kc)
                if kc + 2 < 16:
                    S(kc + 2)
            ri = unit % 2
            rsx = P.op("dve", lambda e, ri=ri, bs=bs: e.reciprocal(out=M_RS[ri], in_=PS[bs][:]), deps=[m2, rs_free[ri]])
            P.bank_free[bs] = rsx
            i, tile, last = BSTP.get()
            o = tt("dve", tile, PS[ba][:], M_RS[ri], ALU.mult, deps=[rsx, last])
            rs_free[ri] = o
            P.bank_free[ba] = o
            BSTP.store(i, aot[h][:, qs], tile, o)
        m_setfree[si] = (P.q["pe"].sem, P.semcnt[P.q["pe"].sem])
    P.rot_banks = list(range(8))
    P.barrier()

    if upto == 3:
        P.barrier()
        P.replay()
        return nc, es, P
    rstate["set_free"] = [None, None]
    rstate["derived"] = None
    for h in range(8):
        ret_head(h, True)
    P.barrier()

    if upto == 4:
        P.barrier()
        P.replay()
        return nc, es, P
    G_RO = carve(RA, 0, [128, 32, T], BF)
    G_AO = carve(RB, 0, [128, 16, T], BF)
    G_G = [carve(RB, 32768 + 4096 * i, [128, 2, T], BF) for i in range(2)]
    gsem = [P.new_sem() for _ in range(2)]
    lro = None
    for i in range(4):
        lro = load(G_RO[:, i * 8:(i + 1) * 8, :], rot[i * 8:(i + 1) * 8].rearrange("c p t -> p c t"), 4 + i)
    lao = load(G_AO, aot.rearrange("c p t -> p c t"), 8)
    g_act = [lao, lro] + [(ldsem[4 + i], P.semcnt[ldsem[4 + i]]) for i in range(4)]
    g_free = [None, None]
    for j in range(32):
        si = j % 2
        GG = G_G[si]
        P.dma("sp", GG[:, 0, :], gtt[j], gsem[si], deps=[g_free[si]])
        lg_ = P.dma("sp", GG[:, 1, :], gtt[32 + j], gsem[si], deps=[g_free[si]])
        hold = {}

        def ev_m(tb, GG=GG, lg_=lg_, hold=hold):
            def evac(ps, tok):
                i, ft, last = FSTP.get()
                a = tt("dve", ft, ps, GG[:, 0, tbs(tb)], ALU.mult, deps=[tok, last, lg_])
                hold[tb] = (i, ft, a)
                return a
            return evac

        def ev_r(tb, GG=GG, lg_=lg_, hold=hold, j=j, si=si):
            def evac(ps, tok):
                i, ft, last = FSTP.get()
                b = tt("dve", ft, ps, GG[:, 1, tbs(tb)], ALU.mult, deps=[tok, last, lg_])
                i0, f0, a = hold[tb]
                i2, tile, l2 = BSTP.get()
                o = tt("dve", tile, f0, ft, ALU.add, deps=[a, b, l2])
                FSTP.release(i0, o)
                FSTP.release(i, o)
                BSTP.store(i2, mixt[j][:, tbs(tb)], tile, o)
                g_free[si] = o
                return b
            return evac

        gemm_job(w_brm[j], 16, [(512, (lambda kc, tb=tb: G_AO[:, kc, tbs(tb)]), ev_m(tb), g_act) for tb in range(NTB)])
        gemm_job(w_brr[j], 32, [(512, (lambda kc, tb=tb: G_RO[:, kc, tbs(tb)]), ev_r(tb), g_act) for tb in range(NTB)])
    P.barrier()

    if upto == 5:
        P.barrier()
        P.replay()
        return nc, es, P
    ACC_S, ACC_Q = STAT[:, 0:T], STAT[:, T:2 * T]
    HALO = sb("HALO", [128, 32], F32)
    HALOB = sb("HALOB", [128, 32], BF)

    def proj_gemm(W, KC, rhs_of, act_deps, res_src, tb_list, st):
        for j in range(32):
            def mk(tb, j=j):
                def evac(ps, tok):
                    i, rt, last = FSTP.get()
                    lr = P.dma("sp", rt, res_src[j][:, tbs(tb)], FSTP.sems[i], deps=[last])
                    y = stt("dve", rt, rt, ALPHA, ps, ALU.mult, ALU.add, deps=[lr, tok])
                    i2, sq, l2 = FSTP.get()
                    a = act(sq, rt, AF.Square, deps=[y, l2])
                    k = ("s", tb)
                    if st.get(k) is None:
                        d1 = P.op("dve", lambda e: e.tensor_copy(out=ACC_S[:, tbs(tb)], in_=rt), deps=[y, st.get("free")])
                        d2 = P.op("dve", lambda e: e.tensor_copy(out=ACC_Q[:, tbs(tb)], in_=sq), deps=[a, st.get("free")])
                    else:
                        d1 = tt("dve", ACC_S[:, tbs(tb)], ACC_S[:, tbs(tb)], rt, ALU.add, deps=[y, st[k]])
                        d2 = tt("dve", ACC_Q[:, tbs(tb)], ACC_Q[:, tbs(tb)], sq, ALU.add, deps=[a, d1])
                    st[k] = d2
                    FSTP.release(i2, d2)
                    P.dma("sp", ysc[j][:, tbs(tb)], rt, FSTP.sems[i], deps=[d2])
                    FSTP.release(i, (FSTP.sems[i], P.semcnt[FSTP.sems[i]]))
                    return y
                return evac
            gemm_job(W[j], KC, [(512, rhs_of(tb), mk(tb), act_deps) for tb in tb_list])

    def ln_stats(st):
        toks = {}
        for tb in range(NTB):
            b = P.next_bank()
            m = mm(PS[b][:], ONESF[:], ACC_S[:, tbs(tb)], True, True, True, deps=[st[("s", tb)], P.bank_free[b], t1])
            mean = ts("dve", ACC_S[:, tbs(tb)], PS[b][:], 1.0 / D, None, ALU.mult, deps=[m])
            P.bank_free[b] = mean
            b2 = P.next_bank()
            m2 = mm(PS[b2][:], ONESF[:], ACC_Q[:, tbs(tb)], True, True, True, deps=[st[("s", tb)], P.bank_free[b2]])
            i, ft, last = FSTP.get()
            sqm = tt("dve", ft, ACC_S[:, tbs(tb)], ACC_S[:, tbs(tb)], ALU.mult, deps=[mean, last])
            var = stt("dve", ACC_Q[:, tbs(tb)], PS[b2][:], 1.0 / D, ft, ALU.mult, ALU.subtract, deps=[m2, sqm])
            FSTP.release(i, var)
            P.bank_free[b2] = var
            sq_ = act(ACC_Q[:, tbs(tb)], ACC_Q[:, tbs(tb)], AF.Sqrt, bias=EPSL, scale=1.0, deps=[var, t_eps])
            toks[tb] = P.op("dve", lambda e, tb=tb: e.reciprocal(out=ACC_Q[:, tbs(tb)], in_=ACC_Q[:, tbs(tb)]),
                            deps=[sq_])
        return toks

    def ln_apply(toks, lni, out_bf, out_bf_dep, out_dram, halo=False):
        P.barrier(queues=("sp",))
        last_tok = None
        items = [(j, tb) for j in range(32) for tb in range(NTB)]
        loaded = {}
        LA = 1

        def issue_load(n):
            j, tb = items[n]
            i, yt, last = FSTP.get()
            ly = P.dma("sp", yt, ysc[j][:, tbs(tb)], FSTP.sems[i], deps=[last])
            loaded[n] = (i, yt, ly)

        for n in range(LA):
            issue_load(n)
        for n, (j, tb) in enumerate(items):
            if n + LA < len(items):
                issue_load(n + LA)
            gcol = VEC[:, V_LN + lni * 64 + j:V_LN + lni * 64 + j + 1]
            bcol = VEC[:, V_LN + lni * 64 + 32 + j:V_LN + lni * 64 + 32 + j + 1]
            i, yt, ly = loaded.pop(n)
            d1 = tt("dve", yt, yt, ACC_S[:, tbs(tb)], ALU.subtract, deps=[ly, toks[tb]])
            d2 = tt("dve", yt, yt, ACC_Q[:, tbs(tb)], ALU.mult, deps=[d1])
            if out_bf is not None:
                act(out_bf[:, j, tbs(tb)], yt, AF.Identity, bias=bcol, scale=gcol, deps=[d2, out_bf_dep, t_vec])
            i2, y2, l2 = FSTP.get()
            a2 = act(y2, yt, AF.Identity, bias=bcol, scale=gcol, deps=[d2, l2, t_vec])
            FSTP.release(i, a2)
            if halo and tb == 1:
                a2 = act(HALO[:, j:j + 1], y2[:, 511:512], AF.Copy, deps=[a2])
            P.dma("sp", out_dram[j][:, tbs(tb)], y2, FSTP.sems[i2], deps=[a2])
            FSTP.release(i2, (FSTP.sems[i2], P.semcnt[FSTP.sems[i2]]))
            last_tok = a2
        return last_tok

    O_MIX = carve(RA, 0, [128, 32, T], BF)
    lmx = None
    for i in range(4):
        lmx = load(O_MIX[:, i * 8:(i + 1) * 8, :], mixt[i * 8:(i + 1) * 8].rearrange("c p t -> p c t"), 4 + i)
    o_deps = [(ldsem[4 + i], P.semcnt[ldsem[4 + i]]) for i in range(4)]
    C_MEM = carve(RB, 0, [128, 32, 256], BF)
    C_CKT = carve(RB, 16384, [128, 32, 256], BF)
    C_CV = carve(RB, 32768, [128, 2, D], BF)
    memsem = P.new_sem()
    P.barrier(queues=("pool",))
    l_mem = P.dma("pool", C_MEM, memT.rearrange("(kc p) t -> p kc t", p=128), memsem)
    st1 = {}
    proj_gemm(w_o, 32, (lambda tb: (lambda kc: O_MIX[:, kc, tbs(tb)])), o_deps, xT.rearrange("(c p) t -> c p t", p=128),
              range(NTB), st1)
    t_wo_done = (P.q["pe"].sem, P.semcnt[P.q["pe"].sem])
    toks1 = ln_stats(st1)
    for j in range(32):
        def ev_ck(ps, tok, j=j):
            return act(C_CKT[:, j, :], ps, AF.Copy, deps=[tok])
        gemm_job(w_ck[j], 32, [(256, (lambda kc: C_MEM[:, kc, :]), ev_ck, [l_mem])])
    CVT = TilePool(P, [carve(RB, 49152 + 512 * i, [128, 256], BF) for i in range(4)], with_sems=False)
    for j in range(32):
        def ev_cv(ps, tok, j=j):
            i, tile, last = CVT.get()
            c = act(tile, ps, AF.Copy, deps=[tok, last])

            def later(i=i, tile=tile, c=c, j=j):
                b = P.next_bank()
                t = None
                for mc in range(2):
                    t = tr(PSB[b][:, mc * 128:(mc + 1) * 128], tile[:, mc * 128:(mc + 1) * 128],
                           deps=[c, P.bank_free[b], t_id], sig=(mc == 1))
                CVT.release(i, t)
                e = P.op("dve", lambda en: en.tensor_copy(
                    out=C_CV[:, :, j * 128:(j + 1) * 128], in_=PSB[b][:, 0:256].rearrange("p (a b) -> p a b", a=2)),
                    deps=[t])
                P.bank_free[b] = e
            P.deferred.append(later)
            return c
        gemm_job(w_cv[j], 32, [(256, (lambda kc: C_MEM[:, kc, :]), ev_cv, [l_mem])])
    flush_deferred()
    H1 = carve(RA, 0, [128, 32, T], BF)
    t_h1 = ln_apply(toks1, 0, H1, t_wo_done, h1s)
    for j in range(32):
        gemm_job(w_cq[j], 32, [(512, (lambda kc, tb=tb: H1[:, kc, tbs(tb)]),
                                ev_store(AF.Copy, (lambda tb, j=j: cqt[j][:, tbs(tb)]))(tb), [t_h1])
                               for tb in range(NTB)])
    P.barrier()

    if upto == 6:
        P.barrier()
        P.replay()
        return nc, es, P
    X_CO = carve(RA, 0, [128, 32, T], BF)
    X_CQ = [carve(RB, 0, [128, 8, T], BF), carve(RB, 49152, [128, 8, T], BF)]
    X_E = [FST[3][:].bitcast(BF), FST[4][:].bitcast(BF)]
    X_RS = FST[2][:]
    xsem = [P.new_sem() for _ in range(2)]
    x_free = [None, None]
    xe_free = [None, None]
    xrs_free = None
    SCALE_X = 1024.0 ** -0.5
    xu = 0
    for hh in range(4):
        si = hh % 2
        lq = P.dma("sp", X_CQ[si], cqt[hh * 8:(hh + 1) * 8].rearrange("c p t -> p c t"), xsem[si], deps=[x_free[si]])
        for qb in range(2):
            qs = tbs(qb)
            ei = xu % 2
            xu += 1
            e_tok = None
            for mc in range(2):
                b = P.next_bank()
                m = None
                for dc in range(8):
                    m = mm(PS[b][:], C_CKT[:, hh * 8 + dc, mc * 128:(mc + 1) * 128], X_CQ[si][:, dc, qs],
                           dc == 0, dc == 7, dc == 7, deps=[lq, P.bank_free[b]] if dc == 0 else [])
                e_tok = act(X_E[ei][:, mc * 512:(mc + 1) * 512], PS[b][:], AF.Exp, scale=SCALE_X,
                            deps=[m, xe_free[ei]])
                P.bank_free[b] = e_tok
            b = P.next_bank()
            mm(PS[b][:], ONESB[:], X_E[ei][:, 0:512], True, False, False, deps=[e_tok, P.bank_free[b]])
            m = mm(PS[b][:], ONESB[:], X_E[ei][:, 512:1024], False, True, True)
            rsx = P.op("dve", lambda e, b=b: e.reciprocal(out=X_RS, in_=PS[b][:]), deps=[m, xrs_free])
            P.bank_free[b] = rsx
            o = None
            for dc in range(8):
                b = P.next_bank()
                col = (hh * 8 + dc) * 128
                mm(PS[b][:], C_CV[:, 0, col:col + 128], X_E[ei][:, 0:512], True, False, False, deps=[P.bank_free[b]])
                m = mm(PS[b][:], C_CV[:, 1, col:col + 128], X_E[ei][:, 512:1024], False, True, True)
                o = tt("dve", X_CO[:, hh * 8 + dc, qs], PS[b][:], X_RS, ALU.mult, deps=[m, rsx])
                P.bank_free[b] = o
            xrs_free = o
            xe_free[ei] = (P.q["pe"].sem, P.semcnt[P.q["pe"].sem])
        x_free[si] = (P.q["pe"].sem, P.semcnt[P.q["pe"].sem])
    P.barrier()

    if upto == 7:
        P.barrier()
        P.replay()
        return nc, es, P
    st2 = {}
    proj_gemm(w_co, 32, (lambda tb: (lambda kc: X_CO[:, kc, tbs(tb)])), [], h1s, range(NTB), st2)
    t_wco_done = (P.q["pe"].sem, P.semcnt[P.q["pe"].sem])
    toks2 = ln_stats(st2)
    H2 = carve(RA, 0, [128, 32, T], BF)
    ln_apply(toks2, 1, H2, t_wco_done, h2s, halo=True)
    P.barrier()

    if upto == 8:
        P.barrier()
        P.replay()
        return nc, es, P
    hsem2 = P.new_sem()
    P.dma("sp", hsrc, HALO[:], hsem2)
    P.barrier()
    t_agh = allgather(hsrc, hdst)
    HG = carve(RB, 0, [128, 2, 32], F32)
    lh = P.dma("sp", HG, hdst.rearrange("(r p) f -> p r f", r=2), hsem2, deps=[t_agh])
    hx = ts("dve", HALO[:], HG[:, 0, :], CONST[:, C_SEL0:C_SEL0 + 1], None, ALU.mult, deps=[lh])
    hx = stt("dve", HALO[:], HG[:, 1, :], CONST[:, C_SEL1:C_SEL1 + 1], HALO[:], ALU.mult, ALU.add, deps=[hx])
    t_halo = P.op("dve", lambda e: e.tensor_copy(out=HALOB[:], in_=HALO[:]), deps=[hx])
    P.barrier()

    if upto == 9:
        P.barrier()
        P.replay()
        return nc, es, P
    F_UB = [[carve(RB, 1024 + (s * 2 + w) * 4608, [128, 1026], F32) for w in range(2)] for s in range(2)]
    F_CV = [[carve(RB, 20480 + (s * 2 + w) * 4096, [128, T], F32) for w in range(2)] for s in range(2)]
    F_OUT = TilePool(P, [carve(RB, 36864 + 2048 * i, [128, T], BF) for i in range(2)])
    z0 = None
    for s_ in range(2):
        for w_ in range(2):
            z0 = P.op("dve", lambda e, s_=s_, w_=w_: e.memset(F_UB[s_][w_][:, 0:1], 0.0))
    ub_free = [[None, None], [None, None]]
    cv_free = [[None, None], [None, None]]
    P.rot_banks = [0, 1, 2, 3, 4, 5, 6]
    P.bank_cur = 0
    for jf in range(NFF):
        s_ = jf % 2
        parts = {}
        for w_ in range(2):
            cidx = 2 * jf + w_
            UBt = F_UB[s_][w_]

            def mk(tb, UBt=UBt, s_=s_, w_=w_):
                def evac(ps, tok):
                    return act(UBt[:, 1 + tb * 512:1 + (tb + 1) * 512], ps, AF.Copy, deps=[tok, ub_free[s_][w_], z0])
                return evac

            def mk_h(UBt=UBt, s_=s_, w_=w_):
                def evac(ps, tok):
                    return act(UBt[:, 1025:1026], ps, AF.Copy, deps=[tok, ub_free[s_][w_]])
                return evac
            groups = [(512, (lambda kc, tb=tb: H2[:, kc, tbs(tb)]), mk(tb), []) for tb in range(NTB)]
            s0, k, wtok, wt = load_w(w_f1[cidx], 32)
            last = None
            for (N, rhs_fn, evac, deps) in groups:
                b = P.next_bank()
                P.wait("pe", wtok)
                P.wait("pe", P.bank_free[b])
                tok = None
                for kc in range(32):
                    tok = mm(PS[b][:, :N], wt[:, kc * 128:(kc + 1) * 128], rhs_fn(kc), kc == 0, kc == 31, kc == 31)
                P.bank_free[b] = evac(PS[b][:, :N], tok)
            P.wait("pe", P.bank_free[7])
            P.wait("pe", t_halo)
            tok = None
            for kc in range(32):
                tok = mm(PS[7][:, 0:1], wt[:, kc * 128:(kc + 1) * 128], HALOB[:, kc:kc + 1], kc == 0, kc == 31, kc == 31)
            hal = mk_h()(PS[7][:, 0:1], tok)
            P.bank_free[7] = hal
            slot_free[s0] = tok
            w0 = VEC[:, V_CW + cidx:V_CW + cidx + 1]
            w1 = VEC[:, V_CW + 172 + cidx:V_CW + 172 + cidx + 1]
            w2 = VEC[:, V_CW + 344 + cidx:V_CW + 344 + cidx + 1]
            cb = VEC[:, V_CB + cidx:V_CB + cidx + 1]
            CVt = F_CV[s_][w_]
            P.wait("dve", P.bank_free[P.rot_banks[(P.bank_cur - 1) % 7]])
            P.wait("dve", P.bank_free[P.rot_banks[(P.bank_cur - 2) % 7]])
            c1 = ts("dve", CVt, UBt[:, 1:1025], w1, cb, ALU.mult, ALU.add, deps=[hal, cv_free[s_][w_], t_vec])
            c2 = stt("dve", CVt, UBt[:, 0:1024], w0, CVt, ALU.mult, ALU.add, deps=[c1])
            c3 = stt("dve", CVt, UBt[:, 2:1026], w2, CVt, ALU.mult, ALU.add, deps=[c2])
            ub_free[s_][w_] = c3
            parts[w_] = c3
        sg = act(F_CV[s_][1], F_CV[s_][1], AF.Silu, deps=[parts[1]])
        i, ot, last = F_OUT.get()
        o = tt("dve", ot, F_CV[s_][0], F_CV[s_][1], ALU.mult, deps=[sg, parts[0], last])
        cv_free[s_] = [o, o]
        F_OUT.store(i, fft[jf], ot, o)
    P.rot_banks = list(range(8))
    P.barrier()

    if upto == 10:
        P.barrier()
        P.replay()
        return nc, es, P
    F_ACT = RAB[:, 0:NFF * 512].rearrange("p (a b) -> p a b", a=NFF)
    st3 = {}
    fsem = P.new_sem()
    for tb in range(NTB):
        P.barrier()
        la = None
        for i0 in range(0, NFF, 22):
            i1 = min(NFF, i0 + 22)
            la = P.dma("sp", F_ACT[:, i0:i1, :], fft[i0:i1, :, tbs(tb)].rearrange("c p t -> p c t"), fsem)
        la = (fsem, P.semcnt[fsem])
        proj_gemm(w_f2, NFF, (lambda tb_: (lambda kc: F_ACT[:, kc, :])), [la], h2s, [tb], st3)
    toks3 = ln_stats(st3)
    ln_apply(toks3, 2, None, None, outT)
    P.barrier()
    P.replay()
    return nc, es, P


def _fm(W, cols=None):
    if cols is not None:
        W = W[:, cols]
    K, N = W.shape
    return np.ascontiguousarray(W.reshape(K // 128, 128, N // 128, 128).transpose(2, 1, 0, 3)).reshape(
        N // 128, 128, K)


_CACHE = {}


def kernel(x, mem, positions, w_in, gate_bias, q_norm_g, w_uq, kv_norm_g, w_ukv, ret_decay_fwd, ret_decay_bwd,
           w_br_mla, w_br_ret, w_o, ln1_g, ln1_b, w_cq, w_ck, w_cv, w_co, ln2_g, ln2_b, w_ffn_in, conv_w, conv_b,
           w_ffn_out, ln3_g, ln3_b):
    f32 = np.float32
    A = lambda a: np.asarray(a)
    x, mem, positions = A(x), A(mem), A(positions)
    kr = list(range(1536, 1600))
    krs = kr[32:] + kr[:32]
    cols = (list(range(0, 1536)) + kr + kr + krs + krs + list(range(1600, 22080)))
    W = {}
    W["w_in"] = _fm(A(w_in)[0], cols)
    cu = []
    for h in range(16):
        cu += list(range(h * 192, h * 192 + 128))
    for j in range(8):
        for h in (2 * j, 2 * j + 1):
            cu += list(range(h * 192 + 128, h * 192 + 192))
    for j in range(8):
        for h in (2 * j, 2 * j + 1):
            cu += list(range(h * 192 + 160, h * 192 + 192)) + list(range(h * 192 + 128, h * 192 + 160))
    W["w_uq"] = _fm(A(w_uq)[0], cu)
    ck = []
    for h in range(16):
        ck += list(range(h * 256, h * 256 + 128))
    for h in range(16):
        ck += list(range(h * 256 + 128, h * 256 + 256))
    W["w_ukv"] = _fm(A(w_ukv)[0], ck)
    W["w_brm"] = _fm(A(w_br_mla)[0])
    W["w_brr"] = _fm(A(w_br_ret)[0])
    W["w_o"] = _fm(A(w_o)[0])
    W["w_cq"] = _fm(A(w_cq)[0])
    W["w_ck"] = _fm(A(w_ck)[0])
    W["w_cv"] = _fm(A(w_cv)[0])
    W["w_co"] = _fm(A(w_co)[0])
    cf = []
    for j in range(NFF):
        cf += list(range(j * 128, (j + 1) * 128)) + list(range(DFF + j * 128, DFF + (j + 1) * 128))
    W["w_f1"] = _fm(A(w_ffn_in)[0], cf)
    W["w_f2"] = _fm(A(w_ffn_out)[0])

    def colT(v):
        v = A(v).reshape(-1)
        return v.reshape(-1, 128).T

    cw = A(conv_w)[0][:, cf]
    cb = A(conv_b)[0][cf]
    lnp = [colT(A(ln1_g)[0]), colT(A(ln1_b)[0]), colT(A(ln2_g)[0]), colT(A(ln2_b)[0]), colT(A(ln3_g)[0]),
           colT(A(ln3_b)[0])]
    p = np.arange(128, dtype=f32)
    n = np.arange(128, dtype=f32)
    const = np.zeros((128, NCONST), f32)
    const[:, C_RP:C_RP + 128] = np.maximum(n[None, :] - p[:, None], 0)
    const[:, C_RM:C_RM + 128] = np.maximum(p[:, None] - n[None, :], 0)
    const[:, C_N1:C_N1 + 128] = (n + 1)[None, :]
    const[:, C_N2:C_N2 + 128] = (128 - n)[None, :]
    const[:, C_MC1] = 127 - p
    const[:, C_MC2] = p
    const[:, C_INV128] = (f32(10000.0) ** (-(np.arange(128, dtype=f32)) / f32(128))).astype(f32)
    const[:, C_INV32] = (f32(10000.0) ** (-(np.arange(128, dtype=f32) % 32) / f32(32))).astype(f32)
    const[:, C_SGN] = np.where((np.arange(128) % 64) < 32, -1.0, 1.0)
    for c_ in range(8):
        const[:, C_T1 + c_] = 127 - p + 128 * (7 - c_)

    in_maps = []
    for c in range(8):
        b, half = c // 2, c % 2
        sl = slice(half * T, (half + 1) * T)
        xs = x[b, sl]
        ps_ = positions[b, sl]
        if half == 1:
            xs = xs[::-1]
            ps_ = ps_[::-1]
        cst = const.copy()
        cst[:, C_SEL0] = 1.0 if half == 1 else 0.0
        cst[:, C_SEL1] = 1.0 if half == 0 else 0.0
        cwc = cw if half == 0 else cw[::-1]
        vec = np.concatenate([colT(A(gate_bias)[0]), colT(A(q_norm_g)[0]), colT(A(kv_norm_g)[0])] + lnp +
                             [colT(cwc[0]), colT(cwc[1]), colT(cwc[2]), colT(cb)], axis=1).astype(f32)
        assert vec.shape == (128, NVEC), vec.shape
        df, db = A(ret_decay_fwd)[0], A(ret_decay_bwd)[0]
        dec = np.concatenate([df, db] if half == 0 else [db, df]).astype(f32)
        m = {
            "xT": np.ascontiguousarray(xs.T),
            "memT": np.ascontiguousarray(mem[b].T),
            "pos": np.ascontiguousarray(np.broadcast_to(ps_.astype(np.int32)[None, :], (128, T))),
            "const": cst,
            "vec": np.ascontiguousarray(vec),
            "decay": np.ascontiguousarray(np.broadcast_to(dec[None, :], (128, 16))),
        }
        m.update(W)
        in_maps.append(m)

    if "nc" not in _CACHE:
        _CACHE["nc"] = build(UPTO)
    nc, es, P = _CACHE["nc"]
    in_maps = [{k: v for k, v in m.items() if k in P.in_names} for m in in_maps]
    res = run_bass_kernel_spmd(nc, in_maps, core_ids=list(range(8)))
    _CACHE["res"] = res
    out = np.empty((4, 2 * T, D), f32)
    for c in range(8):
        b, half = c // 2, c % 2
        o = np.asarray(res.results[c]["outT"]).reshape(D, T).T
        if half == 1:
            o = o[::-1]
        out[b, half * T:(half + 1) * T] = o
    return out
```
